# Optimizing a Trainium2 kernel written in Bass

```python
import math
import jax, jax.numpy as jnp
from jax import lax
import numpy as np

D_MODEL = 1024
BATCH = 16
SEQ = 4096
DEPTH = 4


DA_HEADS = 4
DA_QK_DIM = 64
DA_V_DIM = 2 * DA_QK_DIM
DA_WIDTH = DA_HEADS * DA_V_DIM
Q_BLOCK = 128
HG_HEADS = 4
HG_K = 128
HG_V = 128
HG_KW = HG_HEADS * HG_K
HG_WIDTH = HG_HEADS * HG_V
HG_CHUNK = 64
SG_GROUPS = 8
SG_CHUNK = 128
SG_WIDTH = D_MODEL
SG_GROUP_DIM = SG_WIDTH // SG_GROUPS
REL_BUCKETS = 32
REL_MAX_DIST = 128
D_FF = -(-8 * D_MODEL // (3 * 256)) * 256
N_EVEN = (DEPTH + 1) // 2
N_ODD = DEPTH // 2
EVEN_SPLITS = (DA_WIDTH, DA_WIDTH, DA_WIDTH, HG_KW, HG_KW, HG_KW, HG_WIDTH, HG_WIDTH)
EVEN_IN = sum(EVEN_SPLITS)
ODD_IN = 2 * SG_WIDTH
EPS = 1e-6

kernel_name = 'hybrid_diffattn_hgrn2_sgu_encoder'


def rms_norm(x, gain):
    xf = x.astype(jnp.float32)
    y = xf * lax.rsqrt(jnp.mean(xf * xf, axis=-1, keepdims=True) + EPS)
    return (y * gain.astype(jnp.float32)).astype(x.dtype)


def rel_bucket(rel):
    half = REL_BUCKETS // 2
    max_exact = half // 2
    ret = jnp.where(rel > 0, half, 0)
    n = jnp.abs(rel)
    nf = jnp.maximum(n, 1).astype(jnp.float32)
    large = max_exact + (jnp.log(nf / max_exact) / math.log(REL_MAX_DIST / max_exact)
                         * (half - max_exact)).astype(jnp.int32)
    large = jnp.minimum(large, half - 1)
    return ret + jnp.where(n < max_exact, n, large)


def diff_attention(q, k, v, lam, rel_bias):
    b, s, h, _ = q.shape
    n_blk = s // Q_BLOCK
    q = q * (DA_QK_DIM ** -0.5)
    q_blocks = q.reshape(b, n_blk, Q_BLOCK, h, 2 * DA_QK_DIM).transpose(1, 0, 3, 2, 4)
    k = k.transpose(0, 2, 1, 3)
    k1, k2 = k[..., :DA_QK_DIM], k[..., DA_QK_DIM:]
    v = v.transpose(0, 2, 1, 3)
    k_pos = jnp.arange(s, dtype=jnp.int32)

    def one_block(args):
        qb, blk = args
        q_pos = blk * Q_BLOCK + jnp.arange(Q_BLOCK, dtype=jnp.int32)
        bias = rel_bias[rel_bucket(k_pos[None, :] - q_pos[:, None])]
        bias = bias.astype(jnp.float32).transpose(2, 0, 1)[None]
        s1 = jnp.einsum('bhqd,bhkd->bhqk', qb[..., :DA_QK_DIM], k1).astype(jnp.float32) + bias
        s2 = jnp.einsum('bhqd,bhkd->bhqk', qb[..., DA_QK_DIM:], k2).astype(jnp.float32) + bias
        p = jax.nn.softmax(s1, axis=-1) - lam * jax.nn.softmax(s2, axis=-1)
        return jnp.einsum('bhqk,bhkd->bhqd', p.astype(v.dtype), v)

    o = lax.map(one_block, (q_blocks, jnp.arange(n_blk, dtype=jnp.int32)))
    return o.transpose(1, 0, 3, 2, 4).reshape(b, s, h, DA_V_DIM)


def forget_gate(z, lb):
    z = z.astype(jnp.float32)
    lb = lb.reshape(HG_HEADS, HG_K)
    log_f = jnp.logaddexp(jnp.log(lb), jnp.log1p(-lb) + jax.nn.log_sigmoid(z))
    k = (1.0 - lb) * jax.nn.sigmoid(-z)
    return log_f, k


def chunk_gla(q, k, v, log_f):
    b, s, h, dk = q.shape
    dv = v.shape[-1]
    n = s // HG_CHUNK

    def to_chunks(t):
        return t.reshape(b, n, HG_CHUNK, h, t.shape[-1]).transpose(1, 0, 3, 2, 4)

    qc, kc, vc, lc = to_chunks(q), to_chunks(k), to_chunks(v), to_chunks(log_f)
    mask = jnp.tril(jnp.ones((HG_CHUNK, HG_CHUNK), dtype=bool))[:, :, None]

    def step(state, inp):
        q_, k_, v_, l_ = inp
        cum = jnp.cumsum(l_, axis=-2)
        o_inter = jnp.einsum('bhck,bhkv->bhcv', q_ * jnp.exp(cum), state)
        diff = cum[..., :, None, :] - cum[..., None, :, :]
        decay = jnp.exp(jnp.where(mask, diff, -jnp.inf))
        att = jnp.einsum('bhik,bhjk,bhijk->bhij', q_, k_, decay)
        o = o_inter + jnp.einsum('bhij,bhjv->bhiv', att, v_)
        last = cum[..., -1:, :]
        state = (jnp.exp(last[..., 0, :])[..., None] * state
                 + jnp.einsum('bhck,bhcv->bhkv', k_ * jnp.exp(last - cum), v_))
        return state, o

    s0 = jnp.zeros((b, h, dk, dv), jnp.float32)
    _, o = lax.scan(step, s0, (qc, kc, vc, lc))
    return o.transpose(1, 0, 3, 2, 4).reshape(b, s, h, dv)


def hgrn2_bidir(q, z_fwd, z_bwd, i, lb_fwd, lb_bwd):
    lf_f, k_f = forget_gate(z_fwd, lb_fwd)
    lf_b, k_b = forget_gate(z_bwd, lb_bwd)
    flip = lambda t: jnp.flip(t, axis=1)
    qf = q.astype(jnp.float32)
    vf = i.astype(jnp.float32)
    q2 = jnp.concatenate([qf, flip(qf)], axis=2)
    k2 = jnp.concatenate([k_f, flip(k_b)], axis=2)
    l2 = jnp.concatenate([lf_f, flip(lf_b)], axis=2)
    v2 = jnp.concatenate([vf, flip(vf)], axis=2)
    o2 = chunk_gla(q2, k2, v2, l2)
    return o2[:, :, :HG_HEADS] + flip(o2[:, :, HG_HEADS:])


def even_mixer(h, w_in, w_out, lam, lambda_init, subln, rel_bias, lb_fwd, lb_bwd, hg_norm):
    b, s, _ = h.shape
    da_q, da_k, da_v, hg_q, hg_zf, hg_zb, hg_i, hg_g = jnp.split(
        h @ w_in, np.cumsum(EVEN_SPLITS)[:-1].tolist(), axis=-1)
    o_a = diff_attention(da_q.reshape(b, s, DA_HEADS, 2 * DA_QK_DIM),
                         da_k.reshape(b, s, DA_HEADS, 2 * DA_QK_DIM),
                         da_v.reshape(b, s, DA_HEADS, DA_V_DIM), lam, rel_bias)
    o_a = (rms_norm(o_a, subln) * (1.0 - lambda_init)).reshape(b, s, DA_WIDTH)
    heads = lambda t, d: t.reshape(b, s, HG_HEADS, d)
    o_b = hgrn2_bidir(jax.nn.silu(heads(hg_q, HG_K)), heads(hg_zf, HG_K), heads(hg_zb, HG_K),
                      heads(hg_i, HG_V), lb_fwd, lb_bwd).astype(h.dtype)
    o_b = (rms_norm(o_b, hg_norm) * jax.nn.silu(heads(hg_g, HG_V))).reshape(b, s, HG_WIDTH)
    return jnp.concatenate([o_a, o_b], axis=-1) @ w_out


def odd_mixer(h, w_in, sg_norm, sg_w, sg_b, w_out):
    b, s, _ = h.shape
    u, v = jnp.split(jax.nn.gelu(h @ w_in, approximate=False), 2, axis=-1)
    v = rms_norm(v, sg_norm)
    n = s // SG_CHUNK
    v = v.reshape(b, n, SG_CHUNK, SG_GROUPS, SG_GROUP_DIM)
    v = jnp.einsum('gpq,bnqgc->bnpgc', sg_w, v) + sg_b.T[None, None, :, :, None]
    return (u * v.reshape(b, s, SG_WIDTH)) @ w_out


def swiglu(h, w_in, w_out):
    gate, up = jnp.split(h @ w_in, 2, axis=-1)
    return (jax.nn.silu(gate) * up) @ w_out


def setup_inputs(seed: int = 0) -> dict:
    key = jax.random.key(seed)
    ks = jax.random.split(key, 22)

    def nrm(k, shape, scale):
        return jax.random.normal(k, shape, jnp.float32) * scale

    return {
        'x': nrm(ks[0], (BATCH, SEQ, D_MODEL), 1.0),
        'rel_bias': nrm(ks[1], (REL_BUCKETS, DA_HEADS), 0.5),
        'norm_mix': 1.0 + nrm(ks[2], (DEPTH, D_MODEL), 0.02),
        'norm_ffn': 1.0 + nrm(ks[3], (DEPTH, D_MODEL), 0.02),
        'norm_final': 1.0 + nrm(ks[4], (D_MODEL,), 0.02),
        'w_in_even': nrm(ks[5], (N_EVEN, D_MODEL, EVEN_IN), D_MODEL ** -0.5),
        'w_out_even': nrm(ks[6], (N_EVEN, DA_WIDTH + HG_WIDTH, D_MODEL), (DA_WIDTH + HG_WIDTH) ** -0.5),
        'lambda_q1': nrm(ks[7], (N_EVEN, DA_QK_DIM), 0.1),
        'lambda_k1': nrm(ks[8], (N_EVEN, DA_QK_DIM), 0.1),
        'lambda_q2': nrm(ks[9], (N_EVEN, DA_QK_DIM), 0.1),
        'lambda_k2': nrm(ks[10], (N_EVEN, DA_QK_DIM), 0.1),
        'da_subln': 1.0 + nrm(ks[11], (N_EVEN, DA_V_DIM), 0.02),
        'hg_lb_fwd': nrm(ks[12], (N_EVEN, HG_KW), 0.1),
        'hg_lb_bwd': nrm(ks[13], (N_EVEN, HG_KW), 0.1),
        'hg_norm': 1.0 + nrm(ks[14], (N_EVEN, HG_V), 0.02),
        'w_in_odd': nrm(ks[15], (N_ODD, D_MODEL, ODD_IN), D_MODEL ** -0.5),
        'sg_norm': 1.0 + nrm(ks[16], (N_ODD, SG_WIDTH), 0.02),
        'sg_w': nrm(ks[17], (N_ODD, SG_GROUPS, SG_CHUNK, SG_CHUNK), SG_CHUNK ** -0.5),
        'sg_b': 1.0 + nrm(ks[18], (N_ODD, SG_GROUPS, SG_CHUNK), 0.1),
        'w_out_odd': nrm(ks[19], (N_ODD, SG_WIDTH, D_MODEL), SG_WIDTH ** -0.5),
        'w_ffn_in': nrm(ks[20], (DEPTH, D_MODEL, 2 * D_FF), D_MODEL ** -0.5),
        'w_ffn_out': nrm(ks[21], (DEPTH, D_FF, D_MODEL), D_FF ** -0.5),
    }


def reference(x, rel_bias, norm_mix, norm_ffn, norm_final, w_in_even, w_out_even,
              lambda_q1, lambda_k1, lambda_q2, lambda_k2, da_subln, hg_lb_fwd, hg_lb_bwd, hg_norm,
              w_in_odd, sg_norm, sg_w, sg_b, w_out_odd, w_ffn_in, w_ffn_out):
    f32 = jnp.float32
    lb_f = jnp.cumsum(jax.nn.softmax(hg_lb_fwd.astype(f32), axis=0), axis=0)
    lb_f = lb_f - lb_f[:1]
    lb_b = jnp.cumsum(jax.nn.softmax(hg_lb_bwd.astype(f32), axis=0), axis=0)
    lb_b = lb_b - lb_b[:1]
    h = x
    for l in range(DEPTH):
        hn = rms_norm(h, norm_mix[l])
        if l % 2 == 0:
            e = l // 2
            lambda_init = 0.8 - 0.6 * math.exp(-0.3 * l)
            lam = (jnp.exp(jnp.sum(lambda_q1[e].astype(f32) * lambda_k1[e].astype(f32)))
                   - jnp.exp(jnp.sum(lambda_q2[e].astype(f32) * lambda_k2[e].astype(f32))) + lambda_init)
            h = h + even_mixer(hn, w_in_even[e], w_out_even[e], lam, lambda_init, da_subln[e], rel_bias,
                               lb_f[e], lb_b[e], hg_norm[e])
        else:
            o = l // 2
            h = h + odd_mixer(hn, w_in_odd[o], sg_norm[o], sg_w[o], sg_b[o], w_out_odd[o])
        h = h + swiglu(rms_norm(h, norm_ffn[l]), w_ffn_in[l], w_ffn_out[l])
    return rms_norm(h, norm_final)
```

```python
import math
import os
from contextlib import ExitStack

import numpy as np
import concourse.bass as bass
import concourse.mybir as mybir
from concourse.bass_utils import run_bass_kernel_spmd

F32 = mybir.dt.float32
BF16 = mybir.dt.bfloat16
I32 = mybir.dt.int32
AF = mybir.ActivationFunctionType
ALU = mybir.AluOpType

P = 128
S = 4096
D = 1024
NSEQ = 2
NT = NSEQ * S
TB = 512
NBLK = NT // TB
DFF = 2816
NF = DFF // P
NG = NF // 2
DEPTH = 4
EPS = 1e-6
NCORES = 8


def _rel_bucket_np(rel):
    rel = np.asarray(rel, dtype=np.int32)
    half = 16
    max_exact = 8
    ret = np.where(rel > 0, half, 0).astype(np.int32)
    n = np.abs(rel)
    nf = np.maximum(n, 1).astype(np.float32)
    large = max_exact + (np.log(nf / np.float32(max_exact)) / np.float32(math.log(128 / max_exact))
                         * np.float32(half - max_exact)).astype(np.int32)
    large = np.minimum(large, half - 1)
    return ret + np.where(n < max_exact, n, large)


def _bias_segments():
    rels = np.arange(-700, 701)
    b = _rel_bucket_np(rels)
    first = int(b[0])
    steps = []
    for i in range(1, len(rels)):
        if b[i] != b[i - 1]:
            steps.append((int(rels[i]), int(b[i - 1]), int(b[i])))
    assert abs(steps[0][0]) < 128 and abs(steps[-1][0]) < 128
    return first, steps, int(b[-1])


def _host_consts():
    c = {}
    c["c_ident"] = np.eye(P, dtype=np.float32)
    tri = np.zeros((6, P, P), np.float32)
    tau = np.arange(P)[:, None]
    t = np.arange(P)[None, :]
    same = (tau // 64) == (t // 64)
    tri[0] = (same & (tau <= t))
    tri[1] = (same & (tau > t))
    tri[2] = -tri[1]
    tri[3] = (same & (tau >= t))
    tri[4] = (same & (tau < t))
    tri[5] = -tri[4]
    c["c_tri"] = tri
    jj = np.arange(P)[:, None]
    ii = np.arange(P)[None, :]
    same2 = (jj // 64) == (ii // 64)
    m = np.zeros((2, P, 4, P), np.float32)
    m[0] = (same2 & (jj <= ii))[:, None, :]
    m[1] = (same2 & (jj >= ii))[:, None, :]
    c["c_mask"] = m.reshape(2, P, 512)
    return c


class Sem:
    __slots__ = ("h", "k", "val")

    def __init__(self, h, k):
        self.h = h
        self.k = k
        self.val = 0


class Buf:
    def __init__(self, t, name):
        self.t = t
        self.name = name
        self.w = None
        self.r = {}
        self.ws = None
        self.rs = None

    def __getitem__(self, idx):
        return self.t[idx]


class KB:
    def __init__(self, nc):
        self.nc = nc
        self.eng = {"pe": nc.tensor, "act": nc.scalar, "dve": nc.vector, "pool": nc.gpsimd, "sp": nc.sync}
        self.sems = {}
        self.nsem = 0
        self.esem = {}
        for en in ("pe", "act", "dve", "pool"):
            self.esem[en] = self._new_sem("e_" + en)
        self.seen = {en: {} for en in self.eng}
        self.free_dsems = []
        self.live_dsems = {}
        self.phase_bufs = []
        self.stack = None
        self.n_ins = 0

    def _new_sem(self, name):
        h = self.nc.alloc_semaphore(name)
        s = Sem(h, self.nsem)
        self.sems[s.k] = s
        self.nsem += 1
        return s

    def _dsem(self):
        if self.free_dsems:
            s = self.free_dsems.pop()
        else:
            s = self._new_sem("d%d" % self.nsem)
        self.live_dsems[s.k] = s
        return s

    def sb(self, name, shape, dtype):
        t = self.stack.enter_context(self.nc.sbuf_tensor(name + "_%d" % self.n_alloc(), list(shape), dtype))
        b = Buf(t, name)
        self.phase_bufs.append(b)
        return b

    def ps(self, name, shape, dtype=F32):
        t = self.stack.enter_context(self.nc.psum_tensor(name + "_%d" % self.n_alloc(), list(shape), dtype))
        b = Buf(t, name)
        self.phase_bufs.append(b)
        return b

    def n_alloc(self):
        self._na = getattr(self, "_na", 0) + 1
        return self._na

    def dram_buf(self, ap, name):
        return Buf(ap, name)

    def _deps(self, reads, writes):
        deps = {}
        for b in reads:
            if b.w is not None and deps.get(b.w[0], 0) < b.w[1]:
                deps[b.w[0]] = b.w[1]
        for b in writes:
            if b.w is not None and deps.get(b.w[0], 0) < b.w[1]:
                deps[b.w[0]] = b.w[1]
            for k, v in b.r.items():
                if deps.get(k, 0) < v:
                    deps[k] = v
        return deps

    def _wait(self, en, deps):
        seen = self.seen[en]
        e = self.eng[en]
        own = self.esem[en].k if en in self.esem else -1
        for k, v in deps.items():
            if en == "pe" and k == own:
                continue
            if seen.get(k, 0) >= v:
                continue
            e.wait_ge(self.sems[k].h, v)
            seen[k] = v

    def _record(self, tok, reads, writes):
        k, v = tok
        for b in reads:
            if b.r.get(k, 0) < v:
                b.r[k] = v
        for b in writes:
            b.w = tok
            b.r = {}

    def op(self, en, fn, reads=(), writes=(), sig=True):
        self._wait(en, self._deps(reads, writes))
        ins = fn(self.eng[en])
        s = self.esem[en]
        self.n_ins += 1
        if sig:
            s.val += 1
            ins.then_inc(s.h, 1)
            tok = (s.k, s.val)
        else:
            tok = (s.k, s.val + 1)
        self._record(tok, reads, writes)
        return tok

    def dma(self, qn, pairs, reads=(), writes=(), **kw):
        self._wait(qn, self._deps(reads, writes))
        if writes:
            owner = writes[0]
            if owner.ws is None:
                owner.ws = self._dsem()
            s = owner.ws
        else:
            owner = reads[0]
            if owner.rs is None:
                owner.rs = self._dsem()
            s = owner.rs
        e = self.eng[qn]
        for (o, i) in pairs:
            ins = e.dma_start(out=o, in_=i, **kw)
            s.val += 16
            ins.then_inc(s.h, 16)
            self.n_ins += 1
        tok = (s.k, s.val)
        self._record(tok, reads, writes)
        return tok

    def barrier(self):
        targets = {}
        for en, s in self.esem.items():
            if s.val:
                targets[s.k] = s.val
        for k, s in self.live_dsems.items():
            if s.val:
                targets[k] = s.val
        for en in self.eng:
            self._wait(en, targets)

    def phase(self):
        return _Phase(self)


class _Phase:
    def __init__(self, kb):
        self.kb = kb

    def __enter__(self):
        kb = self.kb
        self.prev = (kb.stack, kb.phase_bufs)
        kb.stack = ExitStack()
        kb.stack.__enter__()
        kb.phase_bufs = []
        return kb

    def __exit__(self, *a):
        kb = self.kb
        kb.barrier()
        for b in kb.phase_bufs:
            for s in (b.ws, b.rs):
                if s is not None:
                    kb.live_dsems.pop(s.k, None)
                    kb.free_dsems.append(s)
        kb.stack.__exit__(None, None, None)
        kb.stack, kb.phase_bufs = self.prev
        return False


class Prog:
    def __init__(self, debug=None, nlayers=DEPTH):
        self.debug = debug or {}
        self.nlayers = nlayers
        nc = bass.Bass("TRN2", target_bir_lowering=False)
        self.nc = nc
        self.kb = KB(nc)

        def ein(name, shape):
            return nc.dram_tensor(name, list(shape), F32, kind="ExternalInput").ap()

        self.x = ein("x", [NT, D])
        self.rel_bias = ein("rel_bias", [32, 4])
        self.norm_mix = ein("norm_mix", [DEPTH, D])
        self.norm_ffn = ein("norm_ffn", [DEPTH, D])
        self.norm_final = ein("norm_final", [1, D])
        self.w_in_even = ein("w_in_even", [2, D, 4096])
        self.w_out_even = ein("w_out_even", [2, D, D])
        self.lambda_q1 = ein("lambda_q1", [2, 64])
        self.lambda_k1 = ein("lambda_k1", [2, 64])
        self.lambda_q2 = ein("lambda_q2", [2, 64])
        self.lambda_k2 = ein("lambda_k2", [2, 64])
        self.da_subln = ein("da_subln", [2, 128])
        self.hg_lb_fwd = ein("hg_lb_fwd", [2, 512])
        self.hg_lb_bwd = ein("hg_lb_bwd", [2, 512])
        self.hg_norm = ein("hg_norm", [2, 128])
        self.w_in_odd = ein("w_in_odd", [2, D, 2048])
        self.sg_norm = ein("sg_norm", [2, D])
        self.sg_w = ein("sg_w", [2, 8, 128, 128])
        self.sg_b = ein("sg_b", [2, 8, 128])
        self.w_out_odd = ein("w_out_odd", [2, D, D])
        self.w_ffn_in = ein("w_ffn_in", [DEPTH, D, 2 * DFF])
        self.w_ffn_out = ein("w_ffn_out", [DEPTH, DFF, D])
        self.c_ident = ein("c_ident", [P, P])
        self.c_tri = ein("c_tri", [6, P, P])
        self.c_mask = ein("c_mask", [2, P, 512])
        self.out = nc.dram_tensor("out", [NT, D], F32, kind="ExternalOutput").ap()

        def scr(name, shape, dt):
            kind = "ExternalOutput" if name in self.debug.get("dump", ()) else "Internal"
            return nc.dram_tensor(name, list(shape), dt, kind=kind).ap()

        self.h_d = scr("h_d", [NT, D], F32)
        self.wmix_in = []
        self.wmix_out = []
        self.wfi = []
        self.wfo = []
        for l in range(DEPTH):
            n_in = 4096 if l % 2 == 0 else 2048
            self.wmix_in.append(scr("wmi%d" % l, [P, 8, n_in], BF16))
            self.wmix_out.append(scr("wmo%d" % l, [P, 8, D], BF16))
            self.wfi.append(scr("wfi%d" % l, [P, 8, 2 * DFF], BF16))
            self.wfo.append(scr("wfo%d" % l, [P, NF, D], BF16))
        self.wl = [self.kb.dram_buf(None, "WLr%d" % l) for l in range(DEPTH)]
        self.wl_mi = [self.kb.dram_buf(None, "WLm%d" % l) for l in range(DEPTH)]
        self.qT_d = scr("qT_d", [4, P, NT], BF16)
        self.kT_d = scr("kT_d", [4, P, NT], BF16)
        self.hq_d = scr("hq_d", [4, P, NT], BF16)
        self.g_d = scr("g_d", [4, P, NT], BF16)
        self.v_d = scr("v_d", [NT, 512], BF16)
        self.i_d = scr("i_d", [NT, 512], BF16)
        self.kf_d = scr("kf_d", [2, NT, 512], BF16)
        self.lf_d = scr("lf_d", [2, NT, 512], F32)
        self.of_d = scr("of_d", [NT // P, P, 512], F32)
        self.o_d = scr("o_d", [D, NT], BF16)
        self.strip_d = scr("strip_d", [4, P, 1152], F32)

    def build(self):
        kb = self.kb
        self.convert_weights(0)
        self.conv_next = 1
        if "layers" in self.debug:
            for l_ in range(1, DEPTH):
                self.convert_weights(l_)
            self.conv_next = DEPTH
        with kb.phase():
            self.setup_consts()
            self.build_strips()
            stop = self.debug.get("stop")
            if stop == "strips":
                return self.nc
            layers = self.debug.get("layers", list(range(DEPTH)))
            for li, l in enumerate(layers):
                src = self.x if li == 0 else self.h_d
                if l % 2 == 0:
                    self.phase_e1(l, src)
                    self.convert_more()
                    if stop == "e1":
                        return self.nc
                    if not self.debug.get("skip_e2"):
                        self.phase_e2(l)
                    if stop == "e2":
                        return self.nc
                    self.phase_e3(l)
                    if stop == "e3":
                        return self.nc
                else:
                    self.phase_odd(l, src)
                last = (li == len(layers) - 1)
                self.convert_more()
                self.phase_ffn(l, src, self.out if last else self.h_d, final=(last and l == DEPTH - 1))
        return self.nc

    def convert_more(self):
        if self.conv_next < DEPTH:
            self.convert_weights(self.conv_next)
            self.conv_next += 1

    def convert_weights(self, l):
        kb = self.kb
        CAP = 2048 * 4
        if l % 2 == 0:
            wi = self.w_in_even[l // 2]
            wo = self.w_out_even[l // 2]
        else:
            wi = self.w_in_odd[l // 2]
            wo = self.w_out_odd[l // 2]
        pairs = [(self.wmix_in[l][:, kc, :], wi[kc * P:(kc + 1) * P, :]) for kc in range(8)]
        kb.dma("pool", pairs, reads=(), writes=(self.wl_mi[l],), max_dma_last_dim=CAP)
        pairs = []
        for kc in range(8):
            pairs.append((self.wmix_out[l][:, kc, :], wo[kc * P:(kc + 1) * P, :]))
        for kc in range(8):
            pairs.append((self.wfi[l][:, kc, :], self.w_ffn_in[l][kc * P:(kc + 1) * P, :]))
        for f in range(NF):
            pairs.append((self.wfo[l][:, f, :], self.w_ffn_out[l][f * P:(f + 1) * P, :]))
        kb.dma("pool", pairs, reads=(), writes=(self.wl[l],), max_dma_last_dim=CAP)

    def setup_consts(self):
        kb = self.kb
        nc = self.nc
        self.ident_f = kb.sb("ident_f", [P, P], F32)
        self.ident_b = kb.sb("ident_b", [P, P], BF16)
        self.ones_b = kb.sb("ones_b", [P, P], BF16)
        self.ones_f = kb.sb("ones_f", [P, P], F32)
        self.eps_col = kb.sb("eps_col", [P, 1], F32)
        self.zero_col = kb.sb("zero_col", [P, 1], F32)
        kb.dma("sp", [(self.ident_f[:], self.c_ident)], writes=(self.ident_f,))
        kb.op("dve", lambda e: e.tensor_copy(out=self.ident_b[:], in_=self.ident_f[:]),
              reads=(self.ident_f,), writes=(self.ident_b,))
        kb.op("dve", lambda e: e.memset(self.ones_b[:], 1.0), writes=(self.ones_b,))
        kb.op("dve", lambda e: e.memset(self.ones_f[:], 1.0), writes=(self.ones_f,))
        kb.op("dve", lambda e: e.memset(self.eps_col[:], EPS), writes=(self.eps_col,))
        kb.op("dve", lambda e: e.memset(self.zero_col[:], 0.0), writes=(self.zero_col,))
        self.rbb = kb.sb("rbb", [P, 128], F32)
        kb.dma("sp", [(self.rbb[:], self.rel_bias.rearrange("b h -> (b h)").partition_broadcast(P))],
               writes=(self.rbb,))

    def norm_T(self, x_ap, xbuf, gain, hnT, col0, W):
        kb = self.kb
        n = W["n"]
        W["n"] += 1
        junk, ss, rt, rstd = W["junk"], W["ss"][n % 2], W["rt"][n % 2], W["rstd"][n % 2]
        hn = W["hn"][n % 2]
        tp = W["tp"][n % 2]
        kb.op("act", lambda e: e.activation(out=junk[:], in_=x_ap, func=AF.Square, accum_out=ss[:]),
              reads=(xbuf,), writes=(junk, ss))
        kb.op("act", lambda e: e.activation(out=rt[:], in_=ss[:], func=AF.Sqrt, bias=self.eps_col[:], scale=1.0 / D),
              reads=(ss, self.eps_col), writes=(rt,))
        kb.op("dve", lambda e: e.reciprocal(out=rstd[:], in_=rt[:]), reads=(rt,), writes=(rstd,))
        kb.op("dve", lambda e: e.scalar_tensor_tensor(out=hn[:], in0=x_ap, scalar=rstd[:, 0:1], in1=gain[:],
                                                      op0=ALU.mult, op1=ALU.mult),
              reads=(xbuf, rstd, gain), writes=(hn,))
        for kc in range(8):
            kb.op("pe", lambda e: e.transpose(out=tp[:, kc, :], in_=hn[:, kc * P:(kc + 1) * P], identity=self.ident_b[:]),
                  reads=(hn, self.ident_b), writes=(tp,), sig=(kc == 7))
        kb.op("act", lambda e: e.copy(out=hnT[:, :, col0:col0 + P], in_=tp[:]), reads=(tp,), writes=(hnT,))

    def norm_work(self):
        kb = self.kb
        return {
            "n": 0,
            "junk": kb.sb("junk", [P, D], BF16),
            "ss": [kb.sb("ss", [P, 1], F32) for _ in range(2)],
            "rt": [kb.sb("rt", [P, 1], F32) for _ in range(2)],
            "rstd": [kb.sb("rstd", [P, 1], F32) for _ in range(2)],
            "hn": [kb.sb("hn", [P, D], BF16) for _ in range(2)],
            "tp": [kb.ps("tp", [P, 8, P], BF16) for _ in range(2)],
        }

    def load_gain(self, row_ap, name):
        g = self.kb.sb(name, [P, D], F32)
        self.kb.dma("sp", [(g[:], row_ap.partition_broadcast(P))], writes=(g,))
        return g

    def load_hblk(self, src, i, hb):
        self.kb.dma("sp", [(hb[:], src[i * TB:(i + 1) * TB, :].rearrange("(s p) d -> p s d", p=P))], writes=(hb,))

    def phase_ffn(self, l, src, dst, final):
        kb = self.kb
        with kb.phase():
            W = self.norm_work()
            g_ffn = self.load_gain(self.norm_ffn[l:l + 1, :], "g_ffn")
            g_fin = self.load_gain(self.norm_final[0:1, :], "g_fin") if final else None
            wo = kb.sb("wo", [P, 8, D], BF16)
            wfo = kb.sb("wfo", [P, NF, D], BF16)
            kb.dma("sp", [(wo[:], self.wmix_out[l])], reads=(self.wl[l],), writes=(wo,))
            hblk = [kb.sb("hblk", [P, 4, D], F32) for _ in range(2)]
            oT = [kb.sb("oT", [P, 8, TB], BF16) for _ in range(2)]
            hnT = kb.sb("hnT", [P, 8, TB], BF16)
            wst = [kb.sb("wst", [P, 8, 512], BF16) for _ in range(3)]
            sgt = [kb.sb("sgt", [P, TB], F32) for _ in range(2)]
            actT = [kb.sb("actT", [P, TB], BF16) for _ in range(NF)]
            ost = [kb.sb("ost", [P, D], F32) for _ in range(2)]
            psA = [kb.ps("psA", [P, TB]) for _ in range(2)]
            psB = [kb.ps("psB", [P, TB]) for _ in range(2)]
            psO = [kb.ps("psO", [P, TB]) for _ in range(2)]
            if final:
                fj = kb.sb("fj", [P, D], BF16)
                fss = kb.sb("fss", [P, 1], F32)
                frt = kb.sb("frt", [P, 1], F32)
                frs = kb.sb("frs", [P, 1], F32)
                ost2 = [kb.sb("ost2", [P, D], F32) for _ in range(2)]
            o_v = self.o_d.rearrange("(kc p) t -> p kc t", p=P)

            def load_blk(i):
                self.load_hblk(src, i, hblk[i % 2])
                kb.dma("sp", [(oT[i % 2][:], o_v[:, :, i * TB:(i + 1) * TB])], writes=(oT[i % 2],))

            nw = [0]
            total_w = NBLK * NG

            def issue_w(upto):
                while nw[0] < min(upto, total_w):
                    n = nw[0]
                    g = n % NG
                    wb = wst[n % 3]
                    kb.dma("sp", [(wb[:, :, 0:256], self.wfi[l][:, :, g * 256:(g + 1) * 256]),
                                  (wb[:, :, 256:512], self.wfi[l][:, :, DFF + g * 256:DFF + (g + 1) * 256])],
                           reads=(self.wl[l],), writes=(wb,))
                    nw[0] += 1

            load_blk(0)
            issue_w(2)
            kb.dma("sp", [(wfo[:], self.wfo[l])], reads=(self.wl[l],), writes=(wfo,))
            no = 0
            npair = 0
            for i in range(NBLK):
                if i + 1 < NBLK:
                    load_blk(i + 1)
                hb = hblk[i % 2]
                ot = oT[i % 2]
                for s in range(4):
                    for hf in range(2):
                        pso = psO[no % 2]
                        no += 1
                        for kc in range(8):
                            kb.op("pe", lambda e: e.matmul(pso[:], lhsT=ot[:, kc, s * P:(s + 1) * P],
                                                           rhs=wo[:, kc, hf * 512:(hf + 1) * 512],
                                                           start=(kc == 0), stop=(kc == 7)),
                                  reads=(ot, wo), writes=(pso,), sig=(kc == 7))
                        kb.op("dve", lambda e: e.tensor_tensor(out=hb[:, s, hf * 512:(hf + 1) * 512],
                                                               in0=hb[:, s, hf * 512:(hf + 1) * 512], in1=pso[:], op=ALU.add),
                              reads=(hb, pso), writes=(hb,))
                    self.norm_T(hb[:, s, :], hb, g_ffn, hnT, s * P, W)
                for g in range(NG):
                    n = i * NG + g
                    issue_w(n + 3)
                    wb = wst[n % 3]
                    for j in range(2):
                        f = 2 * g + j
                        pa = psA[npair % 2]
                        pb = psB[npair % 2]
                        sg = sgt[npair % 2]
                        npair += 1
                        for kc in range(8):
                            kb.op("pe", lambda e: e.matmul(pa[:], lhsT=wb[:, kc, j * P:(j + 1) * P], rhs=hnT[:, kc, :],
                                                           start=(kc == 0), stop=(kc == 7)),
                                  reads=(wb, hnT), writes=(pa,), sig=(kc == 7))
                        for kc in range(8):
                            kb.op("pe", lambda e: e.matmul(pb[:], lhsT=wb[:, kc, 256 + j * P:256 + (j + 1) * P], rhs=hnT[:, kc, :],
                                                           start=(kc == 0), stop=(kc == 7)),
                                  reads=(wb, hnT), writes=(pb,), sig=(kc == 7))
                        kb.op("act", lambda e: e.activation(out=sg[:], in_=pa[:], func=AF.Silu), reads=(pa,), writes=(sg,))
                        kb.op("dve", lambda e: e.tensor_tensor(out=actT[f][:], in0=sg[:], in1=pb[:], op=ALU.mult),
                              reads=(sg, pb), writes=(actT[f],))
                for s in range(4):
                    os_ = ost[(i * 4 + s) % 2]
                    for hf in range(2):
                        pso = psO[no % 2]
                        no += 1
                        for f in range(NF):
                            kb.op("pe", lambda e: e.matmul(pso[:], lhsT=actT[f][:, s * P:(s + 1) * P],
                                                           rhs=wfo[:, f, hf * 512:(hf + 1) * 512],
                                                           start=(f == 0), stop=(f == NF - 1)),
                                  reads=(actT[f], wfo), writes=(pso,), sig=(f == NF - 1))
                        kb.op("dve", lambda e: e.tensor_tensor(out=os_[:, hf * 512:(hf + 1) * 512],
                                                               in0=hb[:, s, hf * 512:(hf + 1) * 512], in1=pso[:], op=ALU.add),
                              reads=(hb, pso), writes=(os_,))
                    row0 = i * TB + s * P
                    if not final:
                        kb.dma("sp", [(dst[row0:row0 + P, :], os_[:])], reads=(os_,))
                    else:
                        o2 = ost2[(i * 4 + s) % 2]
                        kb.op("act", lambda e: e.activation(out=fj[:], in_=os_[:], func=AF.Square, accum_out=fss[:]),
                              reads=(os_,), writes=(fj, fss))
                        kb.op("act", lambda e: e.activation(out=frt[:], in_=fss[:], func=AF.Sqrt, bias=self.eps_col[:], scale=1.0 / D),
                              reads=(fss, self.eps_col), writes=(frt,))
                        kb.op("dve", lambda e: e.reciprocal(out=frs[:], in_=frt[:]), reads=(frt,), writes=(frs,))
                        kb.op("dve", lambda e: e.scalar_tensor_tensor(out=o2[:], in0=os_[:], scalar=frs[:, 0:1], in1=g_fin[:],
                                                                      op0=ALU.mult, op1=ALU.mult),
                              reads=(os_, frs, g_fin), writes=(o2,))
                        kb.dma("sp", [(dst[row0:row0 + P, :], o2[:])], reads=(o2,))

    def phase_odd(self, l, src):
        kb = self.kb
        o = l // 2
        with kb.phase():
            W = self.norm_work()
            g_mix = self.load_gain(self.norm_mix[l:l + 1, :], "g_mix")
            wi = kb.sb("wi", [P, 8, 2048], BF16)
            kb.dma("sp", [(wi[:], self.wmix_in[l])], reads=(self.wl_mi[l],), writes=(wi,))
            g_sg = kb.sb("g_sg", [P, D], F32)
            kb.dma("sp", [(g_sg[:], self.sg_norm[o:o + 1, :].partition_broadcast(P))], writes=(g_sg,))
            sgb = kb.sb("sgb", [P, 8, P], F32)
            kb.dma("sp", [(sgb[:], self.sg_b[o].rearrange("g p -> (g p)").partition_broadcast(P))], writes=(sgb,))
            sgw_n = kb.sb("sgw_n", [P, 8, P], F32)
            kb.dma("sp", [(sgw_n[:], self.sg_w[o].rearrange("g p q -> p g q"))], writes=(sgw_n,))
            sgwT = kb.sb("sgwT", [P, 8, P], BF16)
            psX = [kb.ps("psX", [P, TB]) for _ in range(2)]
            psS = [kb.ps("psS", [P, TB]) for _ in range(2)]
            for g in range(8):
                pt = psX[g % 2]
                kb.op("pe", lambda e: e.transpose(out=pt[:, 0:P], in_=sgw_n[:, g, :], identity=self.ident_f[:]),
                      reads=(sgw_n, self.ident_f), writes=(pt,))
                kb.op("dve", lambda e: e.tensor_copy(out=sgwT[:, g, :], in_=pt[:, 0:P]), reads=(pt,), writes=(sgwT,))
            hblk = [kb.sb("hblk", [P, 4, D], F32) for _ in range(2)]
            hnT = kb.sb("hnT", [P, 8, TB], BF16)
            uT = [kb.sb("uT", [P, TB], F32) for _ in range(8)]
            vg = [kb.sb("vg", [P, D], F32) for _ in range(2)]
            vn = [kb.sb("vn", [P, D], BF16) for _ in range(4)]
            vj = kb.sb("vj", [P, D], BF16)
            vss = [kb.sb("vss", [P, 1], F32) for _ in range(2)]
            vrt = [kb.sb("vrt", [P, 1], F32) for _ in range(2)]
            vrs = [kb.sb("vrs", [P, 1], F32) for _ in range(2)]
            tmp = [kb.sb("tmp", [P, TB], F32) for _ in range(2)]
            gT = [kb.sb("gT", [P, 8, TB], BF16) for _ in range(2)]
            o_v = self.o_d.rearrange("(kc p) t -> p kc t", p=P)
            self.load_hblk(src, 0, hblk[0])
            nx = 0
            nsp = 0
            for i in range(NBLK):
                if i + 1 < NBLK:
                    self.load_hblk(src, i + 1, hblk[(i + 1) % 2])
                hb = hblk[i % 2]
                for s in range(4):
                    self.norm_T(hb[:, s, :], hb, g_mix, hnT, s * P, W)
                for s in range(4):
                    vgs = vg[s % 2]
                    for hf in range(2):
                        px = psX[nx % 2]
                        nx += 1
                        for kc in range(8):
                            kb.op("pe", lambda e: e.matmul(px[:], lhsT=hnT[:, kc, s * P:(s + 1) * P],
                                                           rhs=wi[:, kc, 1024 + hf * 512:1024 + (hf + 1) * 512],
                                                           start=(kc == 0), stop=(kc == 7)),
                                  reads=(hnT, wi), writes=(px,), sig=(kc == 7))
                        kb.op("act", lambda e: e.activation(out=vgs[:, hf * 512:(hf + 1) * 512], in_=px[:], func=AF.Gelu),
                              reads=(px,), writes=(vgs,))
                    a, b, c = vss[s % 2], vrt[s % 2], vrs[s % 2]
                    kb.op("act", lambda e: e.activation(out=vj[:], in_=vgs[:], func=AF.Square, accum_out=a[:]),
                          reads=(vgs,), writes=(vj, a))
                    kb.op("act", lambda e: e.activation(out=b[:], in_=a[:], func=AF.Sqrt, bias=self.eps_col[:], scale=1.0 / D),
                          reads=(a, self.eps_col), writes=(b,))
                    kb.op("dve", lambda e: e.reciprocal(out=c[:], in_=b[:]), reads=(b,), writes=(c,))
                    kb.op("dve", lambda e: e.scalar_tensor_tensor(out=vn[s][:], in0=vgs[:], scalar=c[:, 0:1], in1=g_sg[:],
                                                                  op0=ALU.mult, op1=ALU.mult),
                          reads=(vgs, c, g_sg), writes=(vn[s],))
                for fc in range(8):
                    px = psX[nx % 2]
                    nx += 1
                    for kc in range(8):
                        kb.op("pe", lambda e: e.matmul(px[:], lhsT=wi[:, kc, fc * P:(fc + 1) * P], rhs=hnT[:, kc, :],
                                                       start=(kc == 0), stop=(kc == 7)),
                              reads=(wi, hnT), writes=(px,), sig=(kc == 7))
                    kb.op("act", lambda e: e.activation(out=uT[fc][:], in_=px[:], func=AF.Gelu), reads=(px,), writes=(uT[fc],))
                gt = gT[i % 2]
                for g in range(8):
                    pss = psS[nsp % 2]
                    tm = tmp[nsp % 2]
                    nsp += 1
                    for s in range(4):
                        kb.op("pe", lambda e: e.matmul(pss[:, s * P:(s + 1) * P], lhsT=vn[s][:, g * P:(g + 1) * P],
                                                       rhs=sgwT[:, g, :], start=True, stop=True),
                              reads=(vn[s], sgwT), writes=(pss,), sig=(s == 3))
                    for s in range(4):
                        kb.op("dve", lambda e: e.tensor_tensor(out=tm[:, s * P:(s + 1) * P], in0=pss[:, s * P:(s + 1) * P],
                                                               in1=sgb[:, g, :], op=ALU.add),
                              reads=(pss, sgb), writes=(tm,))
                    kb.op("pool", lambda e: e.tensor_tensor(out=gt[:, g, :], in0=tm[:], in1=uT[g][:], op=ALU.mult),
                          reads=(tm, uT[g]), writes=(gt,))
                kb.dma("sp", [(o_v[:, :, i * TB:(i + 1) * TB], gt[:])], reads=(gt,))

    def build_strips(self):
        kb = self.kb
        first, steps, last = _bias_segments()
        self.b_first, self.b_last = first, last
        with kb.phase():
            XW = 1152
            Ri = kb.sb("Ri", [P, XW], I32)
            R = kb.sb("R", [P, XW], F32)
            kb.op("pool", lambda e: e.iota(Ri[:], pattern=[[-1, XW]], base=512, channel_multiplier=1), writes=(Ri,))
            kb.op("dve", lambda e: e.tensor_copy(out=R[:], in_=Ri[:]), reads=(Ri,), writes=(R,))
            ns = len(steps)
            Dt = kb.sb("Dt", [P, ns * 4], F32)
            for j, (t, bp, bn) in enumerate(steps):
                kb.op("dve", lambda e: e.tensor_tensor(out=Dt[:, j * 4:(j + 1) * 4], in0=self.rbb[:, bn * 4:(bn + 1) * 4],
                                                       in1=self.rbb[:, bp * 4:(bp + 1) * 4], op=ALU.subtract),
                      reads=(self.rbb,), writes=(Dt,))
            accs = [kb.sb("acc", [P, XW], F32) for _ in range(4)]
            msk = [kb.sb("smask", [P, XW], F32) for _ in range(2)]
            for h in range(4):
                kb.op("dve", lambda e: e.tensor_scalar(out=accs[h][:], in0=R[:], scalar1=0.0,
                                                       scalar2=self.rbb[:, first * 4 + h:first * 4 + h + 1],
                                                       op0=ALU.mult, op1=ALU.add),
                      reads=(R, self.rbb), writes=(accs[h],))
            for j, (t, bp, bn) in enumerate(steps):
                mk = msk[j % 2]
                kb.op("dve", lambda e: e.tensor_scalar(out=mk[:], in0=R[:], scalar1=float(t), scalar2=None, op0=ALU.is_ge),
                      reads=(R,), writes=(mk,))
                for h in range(4):
                    kb.op("dve", lambda e: e.scalar_tensor_tensor(out=accs[h][:], in0=mk[:], scalar=Dt[:, j * 4 + h:j * 4 + h + 1],
                                                                  in1=accs[h][:], op0=ALU.mult, op1=ALU.add),
                          reads=(mk, Dt, accs[h]), writes=(accs[h],))
            for h in range(4):
                kb.dma("sp", [(self.strip_d[h], accs[h][:])], reads=(accs[h],))

    def phase_e1(self, l, src):
        kb = self.kb
        e = l // 2
        with kb.phase():
            W = self.norm_work()
            g_mix = self.load_gain(self.norm_mix[l:l + 1, :], "g_mix")
            wi = kb.sb("wi", [P, 8, 4096], BF16)
            kb.dma("sp", [(wi[:], self.wmix_in[l])], reads=(self.wl_mi[l],), writes=(wi,))
            lbt = []
            for d, src_lb in enumerate((self.hg_lb_fwd, self.hg_lb_bwd)):
                lb = kb.sb("lb", [P, 512], F32)
                oml = kb.sb("oml", [P, 512], F32)
                if e == 0:
                    kb.op("dve", lambda e_: e_.memset(lb[:], 0.0), writes=(lb,))
                else:
                    a0 = kb.sb("a0", [P, 512], F32)
                    a1 = kb.sb("a1", [P, 512], F32)
                    kb.dma("sp", [(a0[:], src_lb[0:1, :].partition_broadcast(P))], writes=(a0,))
                    kb.dma("sp", [(a1[:], src_lb[1:2, :].partition_broadcast(P))], writes=(a1,))
                    kb.op("dve", lambda e_: e_.tensor_tensor(out=a1[:], in0=a1[:], in1=a0[:], op=ALU.subtract),
                          reads=(a0, a1), writes=(a1,))
                    kb.op("act", lambda e_: e_.activation(out=lb[:], in_=a1[:], func=AF.Sigmoid), reads=(a1,), writes=(lb,))
                kb.op("dve", lambda e_: e_.tensor_scalar(out=oml[:], in0=lb[:], scalar1=-1.0, scalar2=1.0,
                                                         op0=ALU.mult, op1=ALU.add), reads=(lb,), writes=(oml,))
                lbt.append((lb, oml))
            hblk = [kb.sb("hblk", [P, 4, D], F32) for _ in range(2)]
            hnT = kb.sb("hnT", [P, 8, TB], BF16)
            st_fm = {k: kb.sb("st_" + k, [P, 4, TB], BF16) for k in ("q", "k", "hq", "g")}
            st_v = kb.sb("st_v", [P, 4, 512], BF16)
            st_i = kb.sb("st_i", [P, 4, 512], BF16)
            st_k = [kb.sb("st_kf", [P, 4, 512], BF16) for _ in range(2)]
            st_lf = [kb.sb("st_lf", [P, 4, 512], F32) for _ in range(2)]
            sig = [kb.sb("sig", [P, 512], F32) for _ in range(2)]
            ft = [kb.sb("ft", [P, 512], F32) for _ in range(2)]
            psX = [kb.ps("psX", [P, TB]) for _ in range(4)]
            fm_kinds = (("q", 0, self.qT_d), ("k", 512, self.kT_d), ("hq", 1536, self.hq_d), ("g", 3584, self.g_d))
            self.load_hblk(src, 0, hblk[0])
            nx = 0
            nz = 0
            for i in range(NBLK):
                if i + 1 < NBLK:
                    self.load_hblk(src, i + 1, hblk[(i + 1) % 2])
                hb = hblk[i % 2]
                for s in range(4):
                    self.norm_T(hb[:, s, :], hb, g_mix, hnT, s * P, W)
                for kind, c0, dram in fm_kinds:
                    st = st_fm[kind]
                    for h in range(4):
                        px = psX[nx % 4]
                        nx += 1
                        col = c0 + h * P
                        for kc in range(8):
                            kb.op("pe", lambda e_: e_.matmul(px[:], lhsT=wi[:, kc, col:col + P], rhs=hnT[:, kc, :],
                                                             start=(kc == 0), stop=(kc == 7)),
                                  reads=(wi, hnT), writes=(px,), sig=(kc == 7))
                        if kind == "q":
                            kb.op("dve", lambda e_: e_.tensor_scalar(out=st[:, h, :], in0=px[:], scalar1=0.125, scalar2=None,
                                                                     op0=ALU.mult), reads=(px,), writes=(st,))
                        elif kind == "k":
                            kb.op("dve", lambda e_: e_.tensor_copy(out=st[:, h, :], in_=px[:]), reads=(px,), writes=(st,))
                        else:
                            kb.op("act", lambda e_: e_.activation(out=st[:, h, :], in_=px[:], func=AF.Silu),
                                  reads=(px,), writes=(st,))
                    kb.dma("sp", [(dram.rearrange("h p t -> p h t")[:, :, i * TB:(i + 1) * TB], st[:])], reads=(st,))
                for s in range(4):
                    for kind, c0 in (("v", 1024), ("i", 3072), ("zf", 2048), ("zb", 2560)):
                        px = psX[nx % 4]
                        nx += 1
                        for kc in range(8):
                            kb.op("pe", lambda e_: e_.matmul(px[:], lhsT=hnT[:, kc, s * P:(s + 1) * P], rhs=wi[:, kc, c0:c0 + 512],
                                                             start=(kc == 0), stop=(kc == 7)),
                                  reads=(hnT, wi), writes=(px,), sig=(kc == 7))
                        if kind in ("v", "i"):
                            st = st_v if kind == "v" else st_i
                            kb.op("dve", lambda e_: e_.tensor_copy(out=st[:, s, :], in_=px[:]), reads=(px,), writes=(st,))
                        else:
                            d = 0 if kind == "zf" else 1
                            lb, oml = lbt[d]
                            sg = sig[nz % 2]
                            f_ = ft[nz % 2]
                            nz += 1
                            kb.op("act", lambda e_: e_.activation(out=sg[:], in_=px[:], func=AF.Sigmoid), reads=(px,), writes=(sg,))
                            if e == 0:
                                fsrc = sg
                            else:
                                kb.op("dve", lambda e_: e_.tensor_tensor(out=f_[:], in0=sg[:], in1=oml[:], op=ALU.mult),
                                      reads=(sg, oml), writes=(f_,))
                                kb.op("pool", lambda e_: e_.tensor_tensor(out=f_[:], in0=f_[:], in1=lb[:], op=ALU.add),
                                      reads=(f_, lb), writes=(f_,))
                                fsrc = f_
                            kb.op("act", lambda e_: e_.activation(out=st_lf[d][:, s, :], in_=fsrc[:], func=AF.Ln),
                                  reads=(fsrc,), writes=(st_lf[d],))
                            kb.op("pool", lambda e_: e_.tensor_scalar(out=st_k[d][:, s, :], in0=fsrc[:], scalar1=-1.0, scalar2=1.0,
                                                                      op0=ALU.mult, op1=ALU.add), reads=(fsrc,), writes=(st_k[d],))
                rows = slice(i * TB, (i + 1) * TB)
                kb.dma("sp", [(self.v_d[rows, :].rearrange("(s p) c -> p s c", p=P), st_v[:])], reads=(st_v,))
                kb.dma("sp", [(self.i_d[rows, :].rearrange("(s p) c -> p s c", p=P), st_i[:])], reads=(st_i,))
                for d in range(2):
                    kb.dma("sp", [(self.kf_d[d, rows, :].rearrange("(s p) c -> p s c", p=P), st_k[d][:])], reads=(st_k[d],))
                    kb.dma("sp", [(self.lf_d[d, rows, :].rearrange("(s p) c -> p s c", p=P), st_lf[d][:])], reads=(st_lf[d],))

    def phase_e2(self, l):
        kb = self.kb
        e = l // 2
        lambda_init = 0.8 - 0.6 * math.exp(-0.3 * l)
        with kb.phase():
            strips = []
            for h in range(4):
                sf = kb.sb("strip_f", [P, 1152], F32)
                kb.dma("sp", [(sf[:], self.strip_d[h])], writes=(sf,))
                strips.append(sf)
            lv = []
            for nm, ap in (("q1", self.lambda_q1), ("k1", self.lambda_k1), ("q2", self.lambda_q2), ("k2", self.lambda_k2)):
                t = kb.sb("l" + nm, [P, 64], F32)
                kb.dma("sp", [(t[:], ap[e:e + 1, :].partition_broadcast(P))], writes=(t,))
                lv.append(t)
            es = []
            for a, b in ((lv[0], lv[1]), (lv[2], lv[3])):
                sm = kb.sb("lsum", [P, 1], F32)
                ex = kb.sb("lexp", [P, 1], F32)
                kb.op("dve", lambda e_: e_.tensor_tensor(out=a[:], in0=a[:], in1=b[:], op=ALU.mult), reads=(a, b), writes=(a,))
                kb.op("dve", lambda e_: e_.reduce_sum(out=sm[:], in_=a[:], axis=mybir.AxisListType.X), reads=(a,), writes=(sm,))
                kb.op("act", lambda e_: e_.activation(out=ex[:], in_=sm[:], func=AF.Exp), reads=(sm,), writes=(ex,))
                es.append(ex)
            neglam = kb.sb("neglam", [P, 1], F32)
            kb.op("dve", lambda e_: e_.tensor_tensor(out=neglam[:], in0=es[1][:], in1=es[0][:], op=ALU.subtract),
                  reads=(es[0], es[1]), writes=(neglam,))
            kb.op("dve", lambda e_: e_.tensor_scalar(out=neglam[:], in0=neglam[:], scalar1=-lambda_init, scalar2=None, op0=ALU.add),
                  reads=(neglam,), writes=(neglam,))
            sublnb = kb.sb("sublnb", [P, P], F32)
            kb.dma("sp", [(sublnb[:], self.da_subln[e:e + 1, :].partition_broadcast(P))], writes=(sublnb,))
            kb.op("dve", lambda e_: e_.tensor_scalar(out=sublnb[:], in0=sublnb[:], scalar1=1.0 - lambda_init, scalar2=None, op0=ALU.mult),
                  reads=(sublnb,), writes=(sublnb,))
            VW = 132
            qT = [kb.sb("qT", [P, S], BF16) for _ in range(2)]
            kTa = [kb.sb("kTa", [P, S], BF16) for _ in range(2)]
            kTb = [kb.sb("kTb", [P, S], BF16) for _ in range(2)]
            V = [kb.sb("V", [P, 32, VW], BF16) for _ in range(2)]
            for b_ in range(2):
                kb.op("dve", lambda e_: e_.memset(kTa[b_][64:128, :], 0.0), writes=(kTa[b_],))
                kb.op("dve", lambda e_: e_.memset(kTb[b_][0:64, :], 0.0), writes=(kTb[b_],))
                kb.op("dve", lambda e_: e_.memset(V[b_][:, :, 128:VW], 1.0), writes=(V[b_],))
            P12 = [kb.sb("P12", [P, 2, TB], BF16) for _ in range(4)]
            stmp = [kb.sb("stmp", [P, 2, TB], F32) for _ in range(2)]
            rz = [kb.sb("rz", [P, 4], F32) for _ in range(2)]
            At = [kb.sb("At", [P, P], F32) for _ in range(2)]
            Dt_ = [kb.sb("Dt_", [P, P], F32) for _ in range(2)]
            sqj = kb.sb("sqj", [P, P], BF16)
            ssq = [kb.sb("ssq", [P, 1], F32) for _ in range(2)]
            rt_ = [kb.sb("rt_", [P, 1], F32) for _ in range(2)]
            Dn = [kb.sb("Dn", [P, P], BF16) for _ in range(2)]
            ostg = [kb.sb("ostg", [P, 4, P], BF16) for _ in range(2)]
            psS = [kb.ps("psS", [P, 2, TB]) for _ in range(2)]
            acc = [kb.ps("acc", [P, TB]) for _ in range(3)]
            pT = kb.ps("pT", [P, 4, P], BF16)
            cneg = self.b_first
            cpos = self.b_last
            combos = [(sq, h) for sq in range(NSEQ) for h in range(4)][:self.debug.get('e2_combos', 8)]
            NQT = self.debug.get('e2_qts', 8)

            def region(j, s_):
                idx = j * 2 + s_
                return acc[idx // 3], (idx % 3) * 129, idx

            def load(ci):
                sq, h = combos[ci]
                b = ci % 2
                kb.dma("sp", [(qT[b][:], self.qT_d[h, :, sq * S:(sq + 1) * S])], writes=(qT[b],))
                kb.dma("sp", [(kTa[b][0:64, :], self.kT_d[h, 0:64, sq * S:(sq + 1) * S])], writes=(kTa[b],))
                kb.dma("sp", [(kTb[b][64:128, :], self.kT_d[h, 64:128, sq * S:(sq + 1) * S])], writes=(kTb[b],))
                kb.dma("sp", [(V[b][:, kb4 * 8:(kb4 + 1) * 8, 0:P],
                               self.v_d[sq * S + kb4 * 1024:sq * S + (kb4 + 1) * 1024, h * P:(h + 1) * P].rearrange("(kb p) c -> p kb c", p=P))
                              for kb4 in range(4)], writes=(V[b],))

            load(0)
            nS = 0
            no = 0
            nj = 0
            for ci, (sq, h) in enumerate(combos):
                if ci + 1 < len(combos):
                    load(ci + 1)
                q_, ka_, kb__, v_ = qT[ci % 2], kTa[ci % 2], kTb[ci % 2], V[ci % 2]
                sf = strips[h]
                nmx = [0]

                def emit_S(qt, kbk, n):
                    ps_ = psS[n % 2]
                    d0 = kbk - 4 * qt
                    mixed = (-1 <= d0 <= 4)
                    for (si, kp) in ((0, ka_), (1, kb__)):
                        kb.op("pe", lambda e_: e_.matmul(ps_[:, si, :], lhsT=kp[:, kbk * P:(kbk + 1) * P],
                                                         rhs=q_[:, qt * TB:(qt + 1) * TB], start=True, stop=True),
                              reads=(kp, q_), writes=(ps_,), sig=(si == 1))
                    if not mixed:
                        if d0 < -1:
                            return ps_, self.rbb, cneg * 4 + h
                        return ps_, self.rbb, cpos * 4 + h
                    m = 4 - d0
                    tm = stmp[nmx[0] % 2]
                    nmx[0] += 1
                    for si in range(2):
                        kb.op("dve", lambda e_: e_.tensor_tensor(out=tm[:, si, :], in0=ps_[:, si, :], in1=sf[:, m * P:m * P + TB], op=ALU.add),
                              reads=(ps_, sf), writes=(tm,))
                    return tm, self.zero_col, 0

                def emit_PV(pp_, kbk, pos):
                    for j in range(4):
                        for s_ in range(2):
                            bank, off, idx = region(j, s_)
                            kb.op("pe", lambda e_: e_.matmul(bank[:, off:off + 129], lhsT=pp_[:, s_, j * P:(j + 1) * P],
                                                             rhs=v_[:, kbk, 0:129], start=(pos == 0 and idx % 3 == 0),
                                                             stop=(pos == 31), skip_group_check=True),
                                  reads=(pp_, v_), writes=(bank,), sig=(pos == 31 and idx == 7))

                for qt in range(NQT):
                    order = list(range(32))
                    assert sorted(order) == list(range(32))
                    pend = emit_S(qt, order[0], nS)
                    prev = None
                    for pos in range(32):
                        n = nS
                        nS += 1
                        cur = pend
                        if pos + 1 < 32:
                            pend = emit_S(qt, order[pos + 1], nS)
                        src_, cb, cc = cur
                        pp_ = P12[n % 4]
                        kb.op("act", lambda e_: e_.activation(out=pp_[:], in_=src_[:], func=AF.Exp, bias=cb[:, cc:cc + 1], scale=1.0),
                              reads=(src_, cb), writes=(pp_,))
                        if prev is not None:
                            emit_PV(*prev)
                        prev = (pp_, order[pos], pos)
                    emit_PV(*prev)
                    og = ostg[no % 2]
                    no += 1
                    for j in range(4):
                        b1, o1, _ = region(j, 0)
                        b2, o2, _ = region(j, 1)
                        x = nj % 2
                        nj += 1
                        rz_, A_, D_, sq_, rtt_, dn_ = rz[x], At[x], Dt_[x], ssq[x], rt_[x], Dn[x]
                        kb.op("dve", lambda e_: e_.reciprocal(out=rz_[:, 0:1], in_=b1[:, o1 + 128:o1 + 129]), reads=(b1,), writes=(rz_,))
                        kb.op("dve", lambda e_: e_.reciprocal(out=rz_[:, 1:2], in_=b2[:, o2 + 128:o2 + 129]), reads=(b2,), writes=(rz_,))
                        kb.op("dve", lambda e_: e_.tensor_tensor(out=rz_[:, 2:3], in0=rz_[:, 1:2], in1=neglam[:], op=ALU.mult),
                              reads=(rz_, neglam), writes=(rz_,))
                        kb.op("dve", lambda e_: e_.tensor_scalar(out=A_[:], in0=b1[:, o1:o1 + 128], scalar1=rz_[:, 0:1], scalar2=None, op0=ALU.mult),
                              reads=(b1, rz_), writes=(A_,))
                        kb.op("dve", lambda e_: e_.scalar_tensor_tensor(out=D_[:], in0=b2[:, o2:o2 + 128], scalar=rz_[:, 2:3], in1=A_[:],
                                                                        op0=ALU.mult, op1=ALU.add),
                              reads=(b2, rz_, A_), writes=(D_,))
                        kb.op("act", lambda e_: e_.activation(out=sqj[:], in_=D_[:], func=AF.Square, accum_out=sq_[:]),
                              reads=(D_,), writes=(sqj, sq_))
                        kb.op("act", lambda e_: e_.activation(out=rtt_[:], in_=sq_[:], func=AF.Ln, bias=self.eps_col[:], scale=1.0 / P),
                              reads=(sq_, self.eps_col), writes=(rtt_,))
                        kb.op("act", lambda e_: e_.activation(out=rz_[:, 3:4], in_=rtt_[:], func=AF.Exp, scale=-0.5),
                              reads=(rtt_,), writes=(rz_,))
                        kb.op("dve", lambda e_: e_.scalar_tensor_tensor(out=dn_[:], in0=D_[:], scalar=rz_[:, 3:4], in1=sublnb[:],
                                                                        op0=ALU.mult, op1=ALU.mult),
                              reads=(D_, rz_, sublnb), writes=(dn_,))
                        kb.op("pe", lambda e_: e_.transpose(out=pT[:, j, :], in_=dn_[:], identity=self.ident_b[:]),
                              reads=(dn_, self.ident_b), writes=(pT,))
                    kb.op("act", lambda e_: e_.copy(out=og[:], in_=pT[:]), reads=(pT,), writes=(og,))
                    t0 = sq * S + qt * TB
                    kb.dma("sp", [(self.o_d[h * P:(h + 1) * P, t0:t0 + TB], og[:].rearrange("p j t -> p (j t)"))], reads=(og,))

    def phase_e3(self, l):
        kb = self.kb
        e = l // 2
        with kb.phase():
            tri = kb.sb("tri", [P, 6, P], F32)
            kb.dma("sp", [(tri[:], self.c_tri.rearrange("k a b -> a k b"))], writes=(tri,))
            mask = kb.sb("mask", [P, 2, 512], F32)
            kb.dma("sp", [(mask[:], self.c_mask.rearrange("d p m -> p d m"))], writes=(mask,))
            hgn = kb.sb("hgn", [P, 1], F32)
            kb.dma("sp", [(hgn[:], self.hg_norm[e:e + 1, :].rearrange("o d -> d o"))], writes=(hgn,))
            Sf = [kb.sb("Sf", [P, 4, P], F32) for _ in range(NSEQ)]
            Sb = [kb.sb("Sb", [P, 4, P], BF16) for _ in range(NSEQ)]
            def mk(name, shape, dt, cnt):
                return [kb.sb(name, shape, dt) for _ in range(cnt)]

            def B(lst, n):
                return lst[n % len(lst)]

            lf = mk("lf", [P, 512], F32, 4)
            kk = mk("kk", [P, 512], BF16, 4)
            hq = mk("hq", [P, 4, P], BF16, 4)
            ii = mk("ii", [P, 512], BF16, 8)
            gt = mk("gt", [P, 4, P], BF16, 8)
            ofw = mk("ofw", [P, 512], F32, 8)
            E1t = mk("E1t", [P, 512], F32, 2)
            E2t = mk("E2t", [P, 512], F32, 2)
            E3t = mk("E3t", [P, 512], F32, 2)
            decs = mk("decs", [P, 2, 4], F32, 6)
            kbar = mk("kbar", [P, 512], BF16, 4)
            kbc = [[kb.sb("kbc", [P, 512], BF16) for _ in range(2)] for _ in range(6)]
            for y_ in range(len(kbc)):
                for c_ in range(2):
                    kb.op("dve", lambda e_: e_.memset(kbc[y_][c_][:], 0.0), writes=(kbc[y_][c_],))
            qtil = mk("qtil", [P, 4, P], BF16, 6)
            qhat = mk("qhat", [P, 4, P], BF16, 5)
            khT = mk("khT", [P, 4, P], BF16, 4)
            attm = mk("attm", [P, 512], BF16, 4)
            ofs = mk("ofs", [P, 512], F32, 2)
            tot = kb.sb("tot", [P, 512], F32)
            sqb = kb.sb("sqb", [P, 512], BF16)
            rtt = kb.sb("rtt", [P, 512], F32)
            t2 = kb.sb("t2", [P, 512], F32)
            obs = mk("obs", [P, 4, P], BF16, 2)
            pp = [kb.ps("pp", [P, 512]) for _ in range(2)]
            p_kT = kb.ps("p_kT", [P, 4, P], BF16)
            p_att = kb.ps("p_att", [P, 512])
            p_O = [kb.ps("p_O", [P, 512]) for _ in range(2)]
            p_S = [kb.ps("p_S", [P, 512]) for _ in range(2)]
            npp = [0]

            def nextpp():
                b_ = pp[npp[0] % len(pp)]
                npp[0] += 1
                return b_

            for d in range(2):
                if d == 1:
                    kb.barrier()
                for sq in range(NSEQ):
                    kb.op("dve", lambda e_: e_.memset(Sf[sq][:], 0.0), writes=(Sf[sq],))
                    kb.op("dve", lambda e_: e_.memset(Sb[sq][:], 0.0), writes=(Sb[sq],))
                seqn = [(st if d == 0 else 31 - st, sq) for st in range(32) for sq in range(NSEQ)]
                NN = len(seqn)

                def load(n):
                    b, sq = seqn[n]
                    t0 = sq * S + b * P
                    kb.dma("sp", [(B(lf, n)[:], self.lf_d[d, t0:t0 + P, :])], writes=(B(lf, n),))
                    kb.dma("sp", [(B(kk, n)[:], self.kf_d[d, t0:t0 + P, :])], writes=(B(kk, n),))
                    kb.dma("sp", [(B(ii, n)[:], self.i_d[t0:t0 + P, :])], writes=(B(ii, n),))
                    kb.dma("sp", [(B(hq, n)[:], self.hq_d.rearrange("h p t -> p h t")[:, :, t0:t0 + P])], writes=(B(hq, n),))
                    if d == 1:
                        kb.dma("sp", [(B(gt, n)[:], self.g_d.rearrange("h p t -> p h t")[:, :, t0:t0 + P])], writes=(B(gt, n),))
                        kb.dma("sp", [(B(ofw, n)[:], self.of_d[t0 // P])], writes=(B(ofw, n),))

                def st1(n):
                    lf_, kk_, hq_ = B(lf, n), B(kk, n), B(hq, n)
                    e1, e2, e3 = B(E1t, n), B(E2t, n), B(E3t, n)
                    kb_, qt_, qh_, kc2, dc = B(kbar, n), B(qtil, n), B(qhat, n), B(kbc, n), B(decs, n)
                    p_rev = nextpp()
                    kb.op("pe", lambda e_: e_.matmul(p_rev[:], lhsT=tri[:, 1 + 3 * d, :], rhs=lf_[:], start=True, stop=True),
                          reads=(tri, lf_), writes=(p_rev,))
                    kb.op("act", lambda e_: e_.activation(out=e3[:], in_=p_rev[:], func=AF.Exp), reads=(p_rev,), writes=(e3,))
                    kb.op("dve", lambda e_: e_.tensor_tensor(out=kb_[:], in0=kk_[:], in1=e3[:], op=ALU.mult),
                          reads=(kk_, e3), writes=(kb_,))
                    for c in range(2):
                        kb.op("pool", lambda e_: e_.tensor_copy(out=kc2[c][c * 64:(c + 1) * 64, :], in_=kb_[c * 64:(c + 1) * 64, :]),
                              reads=(kb_,), writes=(kc2[c],))
                    p_cum = nextpp()
                    for h in range(4):
                        kb.op("pe", lambda e_: e_.matmul(p_cum[:, h * P:(h + 1) * P], lhsT=lf_[:, h * P:(h + 1) * P],
                                                         rhs=tri[:, 0 + 3 * d, :], start=True, stop=True),
                              reads=(tri, lf_), writes=(p_cum,), sig=(h == 3))
                    kb.op("act", lambda e_: e_.activation(out=e1[:], in_=p_cum[:], func=AF.Exp), reads=(p_cum,), writes=(e1,))
                    e1v = e1[:].rearrange("p (h t) -> p h t", h=4)
                    kb.op("dve", lambda e_: e_.tensor_tensor(out=qt_[:], in0=hq_[:], in1=e1v, op=ALU.mult),
                          reads=(hq_, e1), writes=(qt_,))
                    for c in range(2):
                        tl = c * 64 + 63 if d == 0 else c * 64
                        kb.op("dve", lambda e_: e_.tensor_copy(out=dc[:, c, :], in_=e1v[:, :, tl]),
                              reads=(e1,), writes=(dc,))
                    p_revT = nextpp()
                    for h in range(4):
                        kb.op("pe", lambda e_: e_.matmul(p_revT[:, h * P:(h + 1) * P], lhsT=lf_[:, h * P:(h + 1) * P],
                                                         rhs=tri[:, 2 + 3 * d, :], start=True, stop=True),
                              reads=(tri, lf_), writes=(p_revT,), sig=(h == 3))
                    kb.op("act", lambda e_: e_.activation(out=e2[:], in_=p_revT[:], func=AF.Exp), reads=(p_revT,), writes=(e2,))
                    kb.op("pool", lambda e_: e_.tensor_tensor(out=qh_[:], in0=hq_[:], in1=e2[:].rearrange("p (h t) -> p h t", h=4), op=ALU.mult),
                          reads=(hq_, e2), writes=(qh_,))

                def st2(n):
                    kb_, kh_ = B(kbar, n), B(khT, n)
                    for h in range(4):
                        kb.op("pe", lambda e_: e_.transpose(out=p_kT[:, h, :], in_=kb_[:, h * P:(h + 1) * P], identity=self.ident_b[:]),
                              reads=(kb_, self.ident_b), writes=(p_kT,), sig=(h == 3))
                    kb.op("act", lambda e_: e_.copy(out=kh_[:], in_=p_kT[:]), reads=(p_kT,), writes=(kh_,))

                def st3(n):
                    kh_, qh_, am = B(khT, n), B(qhat, n), B(attm, n)
                    for h in range(4):
                        kb.op("pe", lambda e_: e_.matmul(p_att[:, h * P:(h + 1) * P], lhsT=kh_[:, h, :], rhs=qh_[:, h, :],
                                                         start=True, stop=True),
                              reads=(kh_, qh_), writes=(p_att,), sig=(h == 3))
                    kb.op("dve", lambda e_: e_.tensor_tensor(out=am[:], in0=p_att[:], in1=mask[:, d, :], op=ALU.mult),
                          reads=(p_att, mask), writes=(am,))

                def chain2(ns):
                    for n in ns:
                        ii_, am, po = B(ii, n), B(attm, n), p_O[n % 2]
                        for h in range(4):
                            kb.op("pe", lambda e_: e_.matmul(po[:, h * P:(h + 1) * P], lhsT=ii_[:, h * P:(h + 1) * P],
                                                             rhs=am[:, h * P:(h + 1) * P], start=(h == 0), stop=False, skip_group_check=True),
                                  reads=(ii_, am), writes=(po,), sig=False)
                    order = (0, 1) if d == 0 else (1, 0)
                    for ci_, c in enumerate(order):
                        r0 = c * 64
                        for n in ns:
                            b, sq = seqn[n]
                            ii_, qt_, po, dc, ps_ = B(ii, n), B(qtil, n), p_O[n % 2], B(decs, n), p_S[sq]
                            kc_ = B(kbc, n)[c]
                            for h in range(4):
                                kb.op("pe", lambda e_: e_.matmul(po[:, h * P + r0:h * P + r0 + 64], lhsT=Sb[sq][:, h, :],
                                                                 rhs=qt_[:, h, r0:r0 + 64], start=False, stop=(ci_ == 1),
                                                                 skip_group_check=True),
                                      reads=(Sb[sq], qt_), writes=(po,), sig=(h == 3))
                            for h in range(4):
                                kb.op("pe", lambda e_: e_.matmul(ps_[:, h * P:(h + 1) * P], lhsT=kc_[:, h * P:(h + 1) * P],
                                                                 rhs=ii_[:, h * P:(h + 1) * P], start=True, stop=True),
                                      reads=(kc_, ii_), writes=(ps_,), sig=(h == 3))
                            for h in range(4):
                                kb.op("dve", lambda e_: e_.scalar_tensor_tensor(out=Sf[sq][:, h, :], in0=Sf[sq][:, h, :],
                                                                                scalar=dc[:, c, h:h + 1],
                                                                                in1=ps_[:, h * P:(h + 1) * P], op0=ALU.mult, op1=ALU.add),
                                      reads=(Sf[sq], dc, ps_), writes=(Sf[sq],))
                            kb.op("act", lambda e_: e_.copy(out=Sb[sq][:], in_=Sf[sq][:]), reads=(Sf[sq],), writes=(Sb[sq],))

                def tail(n):
                    b, sq = seqn[n]
                    po = p_O[n % 2]
                    t0 = sq * S + b * P
                    if d == 0:
                        of_ = ofs[n % 2]
                        kb.op("act", lambda e_: e_.copy(out=of_[:], in_=po[:]), reads=(po,), writes=(of_,))
                        kb.dma("sp", [(self.of_d[t0 // P], of_[:])], reads=(of_,))
                    else:
                        ob = obs[n % 2]
                        ofw_, gt_ = B(ofw, n), B(gt, n)
                        kb.op("dve", lambda e_: e_.tensor_tensor(out=tot[:], in0=po[:], in1=ofw_[:], op=ALU.add),
                              reads=(po, ofw_), writes=(tot,))
                        kb.op("act", lambda e_: e_.activation(out=sqb[:], in_=tot[:], func=AF.Square), reads=(tot,), writes=(sqb,))
                        p_ss = nextpp()
                        kb.op("pe", lambda e_: e_.matmul(p_ss[:], lhsT=self.ones_b[:], rhs=sqb[:], start=True, stop=True),
                              reads=(self.ones_b, sqb), writes=(p_ss,))
                        kb.op("act", lambda e_: e_.activation(out=rtt[:], in_=p_ss[:], func=AF.Ln, bias=self.eps_col[:], scale=1.0 / P),
                              reads=(p_ss, self.eps_col), writes=(rtt,))
                        kb.op("act", lambda e_: e_.activation(out=rtt[:], in_=rtt[:], func=AF.Exp, scale=-0.5), reads=(rtt,), writes=(rtt,))
                        kb.op("dve", lambda e_: e_.scalar_tensor_tensor(out=t2[:], in0=tot[:], scalar=hgn[:, 0:1], in1=rtt[:],
                                                                        op0=ALU.mult, op1=ALU.mult),
                              reads=(tot, hgn, rtt), writes=(t2,))
                        kb.op("pool", lambda e_: e_.tensor_tensor(out=ob[:], in0=t2[:].rearrange("p (h t) -> p h t", h=4), in1=gt_[:], op=ALU.mult),
                              reads=(t2, gt_), writes=(ob,))
                        kb.dma("sp", [(self.o_d[512:1024, t0:t0 + P].rearrange("(h v) t -> v h t", v=P), ob[:])], reads=(ob,))

                for n0 in range(4):
                    load(n0)
                st1(0); st1(1); st1(2)
                st2(0); st2(1)
                st3(0)
                for n in range(NN):
                    if n + 4 < NN:
                        load(n + 4)
                    if n + 3 < NN:
                        st1(n + 3)
                    if n + 2 < NN:
                        st2(n + 2)
                    if n + 1 < NN:
                        st3(n + 1)
                    if n % 2 == 1:
                        chain2((n - 1, n))
                        tail(n - 1)
                        tail(n)


def _run(inputs, prog, core_ids):
    consts = _host_consts()
    x = np.ascontiguousarray(inputs["x"], dtype=np.float32)
    in_maps = []
    for c in core_ids:
        m = {"x": x[2 * c:2 * c + 2].reshape(NT, D)}
        for k, v in inputs.items():
            if k == "x":
                continue
            a = np.ascontiguousarray(v, dtype=np.float32)
            if k == "norm_final":
                a = a.reshape(1, D)
            m[k] = a
        m.update(consts)
        in_maps.append(m)
    return run_bass_kernel_spmd(prog.nc, in_maps, core_ids=list(core_ids))


def kernel(**inputs):
    prog = Prog()
    prog.build()
    res = _run(inputs, prog, list(range(NCORES)))
    out = np.concatenate([r["out"].reshape(2, S, D) for r in res.results], axis=0)
    return out.astype(np.float32)
```

```python
import math
import os
from contextlib import ExitStack

import numpy as np
import concourse.bass as bass
import concourse.mybir as mybir
from concourse.bass_utils import run_bass_kernel_spmd

F32 = mybir.dt.float32
BF16 = mybir.dt.bfloat16
I32 = mybir.dt.int32
AF = mybir.ActivationFunctionType
ALU = mybir.AluOpType

P = 128
S = 4096
D = 1024
NSEQ = 2
NT = NSEQ * S
TB = 512
NBLK = NT // TB
DFF = 2816
NF = DFF // P
NG = NF // 2
DEPTH = 4
EPS = 1e-6
NCORES = 8


def _rel_bucket_np(rel):
    rel = np.asarray(rel, dtype=np.int32)
    half = 16
    max_exact = 8
    ret = np.where(rel > 0, half, 0).astype(np.int32)
    n = np.abs(rel)
    nf = np.maximum(n, 1).astype(np.float32)
    large = max_exact + (np.log(nf / np.float32(max_exact)) / np.float32(math.log(128 / max_exact))
                         * np.float32(half - max_exact)).astype(np.int32)
    large = np.minimum(large, half - 1)
    return ret + np.where(n < max_exact, n, large)


def _bias_segments():
    rels = np.arange(-700, 701)
    b = _rel_bucket_np(rels)
    first = int(b[0])
    steps = []
    for i in range(1, len(rels)):
        if b[i] != b[i - 1]:
            steps.append((int(rels[i]), int(b[i - 1]), int(b[i])))
    assert abs(steps[0][0]) < 128 and abs(steps[-1][0]) < 128
    return first, steps, int(b[-1])


def _host_consts():
    c = {}
    c["c_ident"] = np.eye(P, dtype=np.float32)
    tri = np.zeros((6, P, P), np.float32)
    tau = np.arange(P)[:, None]
    t = np.arange(P)[None, :]
    same = (tau // 64) == (t // 64)
    tri[0] = (same & (tau <= t))
    tri[1] = (same & (tau > t))
    tri[2] = -tri[1]
    tri[3] = (same & (tau >= t))
    tri[4] = (same & (tau < t))
    tri[5] = -tri[4]
    c["c_tri"] = tri
    jj = np.arange(P)[:, None]
    ii = np.arange(P)[None, :]
    same2 = (jj // 64) == (ii // 64)
    m = np.zeros((2, P, 4, P), np.float32)
    m[0] = (same2 & (jj <= ii))[:, None, :]
    m[1] = (same2 & (jj >= ii))[:, None, :]
    c["c_mask"] = m.reshape(2, P, 512)
    return c


class Sem:
    __slots__ = ("h", "k", "val")

    def __init__(self, h, k):
        self.h = h
        self.k = k
        self.val = 0


class Buf:
    def __init__(self, t, name):
        self.t = t
        self.name = name
        self.w = None
        self.r = {}
        self.ws = None
        self.rs = None

    def __getitem__(self, idx):
        return self.t[idx]


class KB:
    def __init__(self, nc):
        self.nc = nc
        self.eng = {"pe": nc.tensor, "act": nc.scalar, "dve": nc.vector, "pool": nc.gpsimd, "sp": nc.sync}
        self.sems = {}
        self.nsem = 0
        self.esem = {}
        for en in ("pe", "act", "dve", "pool"):
            self.esem[en] = self._new_sem("e_" + en)
        self.seen = {en: {} for en in self.eng}
        self.free_dsems = []
        self.live_dsems = {}
        self.phase_bufs = []
        self.stack = None
        self.n_ins = 0

    def _new_sem(self, name):
        h = self.nc.alloc_semaphore(name)
        s = Sem(h, self.nsem)
        self.sems[s.k] = s
        self.nsem += 1
        return s

    def _dsem(self):
        if self.free_dsems:
            s = self.free_dsems.pop()
        else:
            s = self._new_sem("d%d" % self.nsem)
        self.live_dsems[s.k] = s
        return s

    def sb(self, name, shape, dtype):
        t = self.stack.enter_context(self.nc.sbuf_tensor(name + "_%d" % self.n_alloc(), list(shape), dtype))
        b = Buf(t, name)
        self.phase_bufs.append(b)
        return b

    def ps(self, name, shape, dtype=F32):
        t = self.stack.enter_context(self.nc.psum_tensor(name + "_%d" % self.n_alloc(), list(shape), dtype))
        b = Buf(t, name)
        self.phase_bufs.append(b)
        return b

    def n_alloc(self):
        self._na = getattr(self, "_na", 0) + 1
        return self._na

    def dram_buf(self, ap, name):
        return Buf(ap, name)

    def _deps(self, reads, writes):
        deps = {}
        for b in reads:
            if b.w is not None and deps.get(b.w[0], 0) < b.w[1]:
                deps[b.w[0]] = b.w[1]
        for b in writes:
            if b.w is not None and deps.get(b.w[0], 0) < b.w[1]:
                deps[b.w[0]] = b.w[1]
            for k, v in b.r.items():
                if deps.get(k, 0) < v:
                    deps[k] = v
        return deps

    def _wait(self, en, deps):
        seen = self.seen[en]
        e = self.eng[en]
        own = self.esem[en].k if en in self.esem else -1
        for k, v in deps.items():
            if en == "pe" and k == own:
                continue
            if seen.get(k, 0) >= v:
                continue
            e.wait_ge(self.sems[k].h, v)
            seen[k] = v

    def _record(self, tok, reads, writes):
        k, v = tok
        for b in reads:
            if b.r.get(k, 0) < v:
                b.r[k] = v
        for b in writes:
            b.w = tok
            b.r = {}

    def op(self, en, fn, reads=(), writes=(), sig=True):
        self._wait(en, self._deps(reads, writes))
        ins = fn(self.eng[en])
        s = self.esem[en]
        self.n_ins += 1
        if sig:
            s.val += 1
            ins.then_inc(s.h, 1)
            tok = (s.k, s.val)
        else:
            tok = (s.k, s.val + 1)
        self._record(tok, reads, writes)
        return tok

    def dma(self, qn, pairs, reads=(), writes=(), **kw):
        self._wait(qn, self._deps(reads, writes))
        if writes:
            owner = writes[0]
            if owner.ws is None:
                owner.ws = self._dsem()
            s = owner.ws
        else:
            owner = reads[0]
            if owner.rs is None:
                owner.rs = self._dsem()
            s = owner.rs
        e = self.eng[qn]
        for (o, i) in pairs:
            ins = e.dma_start(out=o, in_=i, **kw)
            s.val += 16
            ins.then_inc(s.h, 16)
            self.n_ins += 1
        tok = (s.k, s.val)
        self._record(tok, reads, writes)
        return tok

    def barrier(self):
        targets = {}
        for en, s in self.esem.items():
            if s.val:
                targets[s.k] = s.val
        for k, s in self.live_dsems.items():
            if s.val:
                targets[k] = s.val
        for en in self.eng:
            self._wait(en, targets)

    def phase(self):
        return _Phase(self)


class _Phase:
    def __init__(self, kb):
        self.kb = kb

    def __enter__(self):
        kb = self.kb
        self.prev = (kb.stack, kb.phase_bufs)
        kb.stack = ExitStack()
        kb.stack.__enter__()
        kb.phase_bufs = []
        return kb

    def __exit__(self, *a):
        kb = self.kb
        kb.barrier()
        for b in kb.phase_bufs:
            for s in (b.ws, b.rs):
                if s is not None:
                    kb.live_dsems.pop(s.k, None)
                    kb.free_dsems.append(s)
        kb.stack.__exit__(None, None, None)
        kb.stack, kb.phase_bufs = self.prev
        return False


class Prog:
    def __init__(self, debug=None, nlayers=DEPTH):
        self.debug = debug or {}
        self.nlayers = nlayers
        nc = bass.Bass("TRN2", target_bir_lowering=False)
        self.nc = nc
        self.kb = KB(nc)

        def ein(name, shape):
            return nc.dram_tensor(name, list(shape), F32, kind="ExternalInput").ap()

        self.x = ein("x", [NT, D])
        self.rel_bias = ein("rel_bias", [32, 4])
        self.norm_mix = ein("norm_mix", [DEPTH, D])
        self.norm_ffn = ein("norm_ffn", [DEPTH, D])
        self.norm_final = ein("norm_final", [1, D])
        self.w_in_even = ein("w_in_even", [2, D, 4096])
        self.w_out_even = ein("w_out_even", [2, D, D])
        self.lambda_q1 = ein("lambda_q1", [2, 64])
        self.lambda_k1 = ein("lambda_k1", [2, 64])
        self.lambda_q2 = ein("lambda_q2", [2, 64])
        self.lambda_k2 = ein("lambda_k2", [2, 64])
        self.da_subln = ein("da_subln", [2, 128])
        self.hg_lb_fwd = ein("hg_lb_fwd", [2, 512])
        self.hg_lb_bwd = ein("hg_lb_bwd", [2, 512])
        self.hg_norm = ein("hg_norm", [2, 128])
        self.w_in_odd = ein("w_in_odd", [2, D, 2048])
        self.sg_norm = ein("sg_norm", [2, D])
        self.sg_w = ein("sg_w", [2, 8, 128, 128])
        self.sg_b = ein("sg_b", [2, 8, 128])
        self.w_out_odd = ein("w_out_odd", [2, D, D])
        self.w_ffn_in = ein("w_ffn_in", [DEPTH, D, 2 * DFF])
        self.w_ffn_out = ein("w_ffn_out", [DEPTH, DFF, D])
        self.c_ident = ein("c_ident", [P, P])
        self.c_tri = ein("c_tri", [6, P, P])
        self.c_mask = ein("c_mask", [2, P, 512])
        self.out = nc.dram_tensor("out", [NT, D], F32, kind="ExternalOutput").ap()

        def scr(name, shape, dt):
            kind = "ExternalOutput" if name in self.debug.get("dump", ()) else "Internal"
            return nc.dram_tensor(name, list(shape), dt, kind=kind).ap()

        self.h_d = scr("h_d", [NT, D], F32)
        self.wmix_in = []
        self.wmix_out = []
        self.wfi = []
        self.wfo = []
        for l in range(DEPTH):
            n_in = 4096 if l % 2 == 0 else 2048
            self.wmix_in.append(scr("wmi%d" % l, [P, 8, n_in], BF16))
            self.wmix_out.append(scr("wmo%d" % l, [P, 8, D], BF16))
            self.wfi.append(scr("wfi%d" % l, [P, 8, 2 * DFF], BF16))
            self.wfo.append(scr("wfo%d" % l, [P, NF, D], BF16))
        self.wl = [self.kb.dram_buf(None, "WLr%d" % l) for l in range(DEPTH)]
        self.wl_mi = [self.kb.dram_buf(None, "WLm%d" % l) for l in range(DEPTH)]
        self.qT_d = scr("qT_d", [4, P, NT], BF16)
        self.kT_d = scr("kT_d", [4, P, NT], BF16)
        self.hq_d = scr("hq_d", [4, P, NT], BF16)
        self.g_d = scr("g_d", [4, P, NT], BF16)
        self.v_d = scr("v_d", [NT, 512], BF16)
        self.i_d = scr("i_d", [NT, 512], BF16)
        self.kf_d = scr("kf_d", [2, NT, 512], BF16)
        self.lf_d = scr("lf_d", [2, NT, 512], F32)
        self.of_d = scr("of_d", [NT // P, P, 512], F32)
        self.o_d = scr("o_d", [D, NT], BF16)
        self.strip_d = scr("strip_d", [4, P, 1152], F32)

    def build(self):
        kb = self.kb
        self.convert_weights(0)
        self.conv_next = 1
        if "layers" in self.debug:
            for l_ in range(1, DEPTH):
                self.convert_weights(l_)
            self.conv_next = DEPTH
        with kb.phase():
            self.setup_consts()
            self.build_strips()
            stop = self.debug.get("stop")
            if stop == "strips":
                return self.nc
            layers = self.debug.get("layers", list(range(DEPTH)))
            for li, l in enumerate(layers):
                src = self.x if li == 0 else self.h_d
                if l % 2 == 0:
                    self.phase_e1(l, src)
                    self.convert_more()
                    if stop == "e1":
                        return self.nc
                    if not self.debug.get("skip_e2"):
                        self.phase_e2(l)
                    if stop == "e2":
                        return self.nc
                    self.phase_e3(l)
                    if stop == "e3":
                        return self.nc
                else:
                    self.phase_odd(l, src)
                last = (li == len(layers) - 1)
                self.convert_more()
                self.phase_ffn(l, src, self.out if last else self.h_d, final=(last and l == DEPTH - 1))
        return self.nc

    def convert_more(self):
        if self.conv_next < DEPTH:
            self.convert_weights(self.conv_next)
            self.conv_next += 1

    def convert_weights(self, l):
        kb = self.kb
        CAP = 2048 * 4
        if l % 2 == 0:
            wi = self.w_in_even[l // 2]
            wo = self.w_out_even[l // 2]
        else:
            wi = self.w_in_odd[l // 2]
            wo = self.w_out_odd[l // 2]
        pairs = [(self.wmix_in[l][:, kc, :], wi[kc * P:(kc + 1) * P, :]) for kc in range(8)]
        kb.dma("pool", pairs, reads=(), writes=(self.wl_mi[l],), max_dma_last_dim=CAP)
        pairs = []
        for kc in range(8):
            pairs.append((self.wmix_out[l][:, kc, :], wo[kc * P:(kc + 1) * P, :]))
        for kc in range(8):
            pairs.append((self.wfi[l][:, kc, :], self.w_ffn_in[l][kc * P:(kc + 1) * P, :]))
        for f in range(NF):
            pairs.append((self.wfo[l][:, f, :], self.w_ffn_out[l][f * P:(f + 1) * P, :]))
        kb.dma("pool", pairs, reads=(), writes=(self.wl[l],), max_dma_last_dim=CAP)

    def setup_consts(self):
        kb = self.kb
        nc = self.nc
        self.ident_f = kb.sb("ident_f", [P, P], F32)
        self.ident_b = kb.sb("ident_b", [P, P], BF16)
        self.ones_b = kb.sb("ones_b", [P, P], BF16)
        self.ones_f = kb.sb("ones_f", [P, P], F32)
        self.eps_col = kb.sb("eps_col", [P, 1], F32)
        self.zero_col = kb.sb("zero_col", [P, 1], F32)
        kb.dma("sp", [(self.ident_f[:], self.c_ident)], writes=(self.ident_f,))
        kb.op("dve", lambda e: e.tensor_copy(out=self.ident_b[:], in_=self.ident_f[:]),
              reads=(self.ident_f,), writes=(self.ident_b,))
        kb.op("dve", lambda e: e.memset(self.ones_b[:], 1.0), writes=(self.ones_b,))
        kb.op("dve", lambda e: e.memset(self.ones_f[:], 1.0), writes=(self.ones_f,))
        kb.op("dve", lambda e: e.memset(self.eps_col[:], EPS), writes=(self.eps_col,))
        kb.op("dve", lambda e: e.memset(self.zero_col[:], 0.0), writes=(self.zero_col,))
        self.rbb = kb.sb("rbb", [P, 128], F32)
        kb.dma("sp", [(self.rbb[:], self.rel_bias.rearrange("b h -> (b h)").partition_broadcast(P))],
               writes=(self.rbb,))

    def norm_T(self, x_ap, xbuf, gain, hnT, col0, W):
        kb = self.kb
        n = W["n"]
        W["n"] += 1
        junk, ss, rt, rstd = W["junk"], W["ss"][n % 2], W["rt"][n % 2], W["rstd"][n % 2]
        hn = W["hn"][n % 2]
        tp = W["tp"][n % 2]
        kb.op("act", lambda e: e.activation(out=junk[:], in_=x_ap, func=AF.Square, accum_out=ss[:]),
              reads=(xbuf,), writes=(junk, ss))
        kb.op("act", lambda e: e.activation(out=rt[:], in_=ss[:], func=AF.Sqrt, bias=self.eps_col[:], scale=1.0 / D),
              reads=(ss, self.eps_col), writes=(rt,))
        kb.op("dve", lambda e: e.reciprocal(out=rstd[:], in_=rt[:]), reads=(rt,), writes=(rstd,))
        kb.op("dve", lambda e: e.scalar_tensor_tensor(out=hn[:], in0=x_ap, scalar=rstd[:, 0:1], in1=gain[:],
                                                      op0=ALU.mult, op1=ALU.mult),
              reads=(xbuf, rstd, gain), writes=(hn,))
        for kc in range(8):
            kb.op("pe", lambda e: e.transpose(out=tp[:, kc, :], in_=hn[:, kc * P:(kc + 1) * P], identity=self.ident_b[:]),
                  reads=(hn, self.ident_b), writes=(tp,), sig=(kc == 7))
        kb.op("act", lambda e: e.copy(out=hnT[:, :, col0:col0 + P], in_=tp[:]), reads=(tp,), writes=(hnT,))

    def norm_work(self):
        kb = self.kb
        return {
            "n": 0,
            "junk": kb.sb("junk", [P, D], BF16),
            "ss": [kb.sb("ss", [P, 1], F32) for _ in range(2)],
            "rt": [kb.sb("rt", [P, 1], F32) for _ in range(2)],
            "rstd": [kb.sb("rstd", [P, 1], F32) for _ in range(2)],
            "hn": [kb.sb("hn", [P, D], BF16) for _ in range(2)],
            "tp": [kb.ps("tp", [P, 8, P], BF16) for _ in range(2)],
        }

    def load_gain(self, row_ap, name):
        g = self.kb.sb(name, [P, D], F32)
        self.kb.dma("sp", [(g[:], row_ap.partition_broadcast(P))], writes=(g,))
        return g

    def load_hblk(self, src, i, hb):
        self.kb.dma("sp", [(hb[:], src[i * TB:(i + 1) * TB, :].rearrange("(s p) d -> p s d", p=P))], writes=(hb,))

    def phase_ffn(self, l, src, dst, final):
        kb = self.kb
        with kb.phase():
            W = self.norm_work()
            g_ffn = self.load_gain(self.norm_ffn[l:l + 1, :], "g_ffn")
            g_fin = self.load_gain(self.norm_final[0:1, :], "g_fin") if final else None
            wo = kb.sb("wo", [P, 8, D], BF16)
            wfo = kb.sb("wfo", [P, NF, D], BF16)
            kb.dma("sp", [(wo[:], self.wmix_out[l])], reads=(self.wl[l],), writes=(wo,))
            hblk = [kb.sb("hblk", [P, 4, D], F32) for _ in range(2)]
            oT = [kb.sb("oT", [P, 8, TB], BF16) for _ in range(2)]
            hnT = kb.sb("hnT", [P, 8, TB], BF16)
            wst = [kb.sb("wst", [P, 8, 512], BF16) for _ in range(3)]
            sgt = [kb.sb("sgt", [P, TB], F32) for _ in range(2)]
            actT = [kb.sb("actT", [P, TB], BF16) for _ in range(NF)]
            ost = [kb.sb("ost", [P, D], F32) for _ in range(2)]
            psA = [kb.ps("psA", [P, TB]) for _ in range(2)]
            psB = [kb.ps("psB", [P, TB]) for _ in range(2)]
            psO = [kb.ps("psO", [P, TB]) for _ in range(2)]
            if final:
                fj = kb.sb("fj", [P, D], BF16)
                fss = kb.sb("fss", [P, 1], F32)
                frt = kb.sb("frt", [P, 1], F32)
                frs = kb.sb("frs", [P, 1], F32)
                ost2 = [kb.sb("ost2", [P, D], F32) for _ in range(2)]
            o_v = self.o_d.rearrange("(kc p) t -> p kc t", p=P)

            def load_blk(i):
                self.load_hblk(src, i, hblk[i % 2])
                kb.dma("sp", [(oT[i % 2][:], o_v[:, :, i * TB:(i + 1) * TB])], writes=(oT[i % 2],))

            nw = [0]
            total_w = NBLK * NG

            def issue_w(upto):
                while nw[0] < min(upto, total_w):
                    n = nw[0]
                    g = n % NG
                    wb = wst[n % 3]
                    kb.dma("sp", [(wb[:, :, 0:256], self.wfi[l][:, :, g * 256:(g + 1) * 256]),
                                  (wb[:, :, 256:512], self.wfi[l][:, :, DFF + g * 256:DFF + (g + 1) * 256])],
                           reads=(self.wl[l],), writes=(wb,))
                    nw[0] += 1

            load_blk(0)
            issue_w(2)
            kb.dma("sp", [(wfo[:], self.wfo[l])], reads=(self.wl[l],), writes=(wfo,))
            no = 0
            npair = 0
            for i in range(NBLK):
                if i + 1 < NBLK:
                    load_blk(i + 1)
                hb = hblk[i % 2]
                ot = oT[i % 2]
                for s in range(4):
                    for hf in range(2):
                        pso = psO[no % 2]
                        no += 1
                        for kc in range(8):
                            kb.op("pe", lambda e: e.matmul(pso[:], lhsT=ot[:, kc, s * P:(s + 1) * P],
                                                           rhs=wo[:, kc, hf * 512:(hf + 1) * 512],
                                                           start=(kc == 0), stop=(kc == 7)),
                                  reads=(ot, wo), writes=(pso,), sig=(kc == 7))
                        kb.op("dve", lambda e: e.tensor_tensor(out=hb[:, s, hf * 512:(hf + 1) * 512],
                                                               in0=hb[:, s, hf * 512:(hf + 1) * 512], in1=pso[:], op=ALU.add),
                              reads=(hb, pso), writes=(hb,))
                    self.norm_T(hb[:, s, :], hb, g_ffn, hnT, s * P, W)
                for g in range(NG):
                    n = i * NG + g
                    issue_w(n + 3)
                    wb = wst[n % 3]
                    for j in range(2):
                        f = 2 * g + j
                        pa = psA[npair % 2]
                        pb = psB[npair % 2]
                        sg = sgt[npair % 2]
                        npair += 1
                        for kc in range(8):
                            kb.op("pe", lambda e: e.matmul(pa[:], lhsT=wb[:, kc, j * P:(j + 1) * P], rhs=hnT[:, kc, :],
                                                           start=(kc == 0), stop=(kc == 7)),
                                  reads=(wb, hnT), writes=(pa,), sig=(kc == 7))
                        for kc in range(8):
                            kb.op("pe", lambda e: e.matmul(pb[:], lhsT=wb[:, kc, 256 + j * P:256 + (j + 1) * P], rhs=hnT[:, kc, :],
                                                           start=(kc == 0), stop=(kc == 7)),
                                  reads=(wb, hnT), writes=(pb,), sig=(kc == 7))
                        kb.op("act", lambda e: e.activation(out=sg[:], in_=pa[:], func=AF.Silu), reads=(pa,), writes=(sg,))
                        kb.op("dve", lambda e: e.tensor_tensor(out=actT[f][:], in0=sg[:], in1=pb[:], op=ALU.mult),
                              reads=(sg, pb), writes=(actT[f],))
                for s in range(4):
                    os_ = ost[(i * 4 + s) % 2]
                    for hf in range(2):
                        pso = psO[no % 2]
                        no += 1
                        for f in range(NF):
                            kb.op("pe", lambda e: e.matmul(pso[:], lhsT=actT[f][:, s * P:(s + 1) * P],
                                                           rhs=wfo[:, f, hf * 512:(hf + 1) * 512],
                                                           start=(f == 0), stop=(f == NF - 1)),
                                  reads=(actT[f], wfo), writes=(pso,), sig=(f == NF - 1))
                        kb.op("dve", lambda e: e.tensor_tensor(out=os_[:, hf * 512:(hf + 1) * 512],
                                                               in0=hb[:, s, hf * 512:(hf + 1) * 512], in1=pso[:], op=ALU.add),
                              reads=(hb, pso), writes=(os_,))
                    row0 = i * TB + s * P
                    if not final:
                        kb.dma("sp", [(dst[row0:row0 + P, :], os_[:])], reads=(os_,))
                    else:
                        o2 = ost2[(i * 4 + s) % 2]
                        kb.op("act", lambda e: e.activation(out=fj[:], in_=os_[:], func=AF.Square, accum_out=fss[:]),
                              reads=(os_,), writes=(fj, fss))
                        kb.op("act", lambda e: e.activation(out=frt[:], in_=fss[:], func=AF.Sqrt, bias=self.eps_col[:], scale=1.0 / D),
                              reads=(fss, self.eps_col), writes=(frt,))
                        kb.op("dve", lambda e: e.reciprocal(out=frs[:], in_=frt[:]), reads=(frt,), writes=(frs,))
                        kb.op("dve", lambda e: e.scalar_tensor_tensor(out=o2[:], in0=os_[:], scalar=frs[:, 0:1], in1=g_fin[:],
                                                                      op0=ALU.mult, op1=ALU.mult),
                              reads=(os_, frs, g_fin), writes=(o2,))
                        kb.dma("sp", [(dst[row0:row0 + P, :], o2[:])], reads=(o2,))

    def phase_odd(self, l, src):
        kb = self.kb
        o = l // 2
        with kb.phase():
            W = self.norm_work()
            g_mix = self.load_gain(self.norm_mix[l:l + 1, :], "g_mix")
            wi = kb.sb("wi", [P, 8, 2048], BF16)
            kb.dma("sp", [(wi[:], self.wmix_in[l])], reads=(self.wl_mi[l],), writes=(wi,))
            g_sg = kb.sb("g_sg", [P, D], F32)
            kb.dma("sp", [(g_sg[:], self.sg_norm[o:o + 1, :].partition_broadcast(P))], writes=(g_sg,))
            sgb = kb.sb("sgb", [P, 8, P], F32)
            kb.dma("sp", [(sgb[:], self.sg_b[o].rearrange("g p -> (g p)").partition_broadcast(P))], writes=(sgb,))
            sgw_n = kb.sb("sgw_n", [P, 8, P], F32)
            kb.dma("sp", [(sgw_n[:], self.sg_w[o].rearrange("g p q -> p g q"))], writes=(sgw_n,))
            sgwT = kb.sb("sgwT", [P, 8, P], BF16)
            psX = [kb.ps("psX", [P, TB]) for _ in range(2)]
            psS = [kb.ps("psS", [P, TB]) for _ in range(2)]
            for g in range(8):
                pt = psX[g % 2]
                kb.op("pe", lambda e: e.transpose(out=pt[:, 0:P], in_=sgw_n[:, g, :], identity=self.ident_f[:]),
                      reads=(sgw_n, self.ident_f), writes=(pt,))
                kb.op("dve", lambda e: e.tensor_copy(out=sgwT[:, g, :], in_=pt[:, 0:P]), reads=(pt,), writes=(sgwT,))
            hblk = [kb.sb("hblk", [P, 4, D], F32) for _ in range(2)]
            hnT = kb.sb("hnT", [P, 8, TB], BF16)
            uT = [kb.sb("uT", [P, TB], F32) for _ in range(8)]
            vg = [kb.sb("vg", [P, D], F32) for _ in range(2)]
            vn = [kb.sb("vn", [P, D], BF16) for _ in range(4)]
            vj = kb.sb("vj", [P, D], BF16)
            vss = [kb.sb("vss", [P, 1], F32) for _ in range(2)]
            vrt = [kb.sb("vrt", [P, 1], F32) for _ in range(2)]
            vrs = [kb.sb("vrs", [P, 1], F32) for _ in range(2)]
            tmp = [kb.sb("tmp", [P, TB], F32) for _ in range(2)]
            gT = [kb.sb("gT", [P, 8, TB], BF16) for _ in range(2)]
            o_v = self.o_d.rearrange("(kc p) t -> p kc t", p=P)
            self.load_hblk(src, 0, hblk[0])
            nx = 0
            nsp = 0
            for i in range(NBLK):
                if i + 1 < NBLK:
                    self.load_hblk(src, i + 1, hblk[(i + 1) % 2])
                hb = hblk[i % 2]
                for s in range(4):
                    self.norm_T(hb[:, s, :], hb, g_mix, hnT, s * P, W)
                for s in range(4):
                    vgs = vg[s % 2]
                    for hf in range(2):
                        px = psX[nx % 2]
                        nx += 1
                        for kc in range(8):
                            kb.op("pe", lambda e: e.matmul(px[:], lhsT=hnT[:, kc, s * P:(s + 1) * P],
                                                           rhs=wi[:, kc, 1024 + hf * 512:1024 + (hf + 1) * 512],
                                                           start=(kc == 0), stop=(kc == 7)),
                                  reads=(hnT, wi), writes=(px,), sig=(kc == 7))
                        kb.op("act", lambda e: e.activation(out=vgs[:, hf * 512:(hf + 1) * 512], in_=px[:], func=AF.Gelu),
                              reads=(px,), writes=(vgs,))
                    a, b, c = vss[s % 2], vrt[s % 2], vrs[s % 2]
                    kb.op("act", lambda e: e.activation(out=vj[:], in_=vgs[:], func=AF.Square, accum_out=a[:]),
                          reads=(vgs,), writes=(vj, a))
                    kb.op("act", lambda e: e.activation(out=b[:], in_=a[:], func=AF.Sqrt, bias=self.eps_col[:], scale=1.0 / D),
                          reads=(a, self.eps_col), writes=(b,))
                    kb.op("dve", lambda e: e.reciprocal(out=c[:], in_=b[:]), reads=(b,), writes=(c,))
                    kb.op("dve", lambda e: e.scalar_tensor_tensor(out=vn[s][:], in0=vgs[:], scalar=c[:, 0:1], in1=g_sg[:],
                                                                  op0=ALU.mult, op1=ALU.mult),
                          reads=(vgs, c, g_sg), writes=(vn[s],))
                for fc in range(8):
                    px = psX[nx % 2]
                    nx += 1
                    for kc in range(8):
                        kb.op("pe", lambda e: e.matmul(px[:], lhsT=wi[:, kc, fc * P:(fc + 1) * P], rhs=hnT[:, kc, :],
                                                       start=(kc == 0), stop=(kc == 7)),
                              reads=(wi, hnT), writes=(px,), sig=(kc == 7))
                    kb.op("act", lambda e: e.activation(out=uT[fc][:], in_=px[:], func=AF.Gelu), reads=(px,), writes=(uT[fc],))
                gt = gT[i % 2]
                for g in range(8):
                    pss = psS[nsp % 2]
                    tm = tmp[nsp % 2]
                    nsp += 1
                    for s in range(4):
                        kb.op("pe", lambda e: e.matmul(pss[:, s * P:(s + 1) * P], lhsT=vn[s][:, g * P:(g + 1) * P],
                                                       rhs=sgwT[:, g, :], start=True, stop=True),
                              reads=(vn[s], sgwT), writes=(pss,), sig=(s == 3))
                    for s in range(4):
                        kb.op("dve", lambda e: e.tensor_tensor(out=tm[:, s * P:(s + 1) * P], in0=pss[:, s * P:(s + 1) * P],
                                                               in1=sgb[:, g, :], op=ALU.add),
                              reads=(pss, sgb), writes=(tm,))
                    kb.op("pool", lambda e: e.tensor_tensor(out=gt[:, g, :], in0=tm[:], in1=uT[g][:], op=ALU.mult),
                          reads=(tm, uT[g]), writes=(gt,))
                kb.dma("sp", [(o_v[:, :, i * TB:(i + 1) * TB], gt[:])], reads=(gt,))

    def build_strips(self):
        kb = self.kb
        first, steps, last = _bias_segments()
        self.b_first, self.b_last = first, last
        with kb.phase():
            XW = 1152
            Ri = kb.sb("Ri", [P, XW], I32)
            R = kb.sb("R", [P, XW], F32)
            kb.op("pool", lambda e: e.iota(Ri[:], pattern=[[-1, XW]], base=512, channel_multiplier=1), writes=(Ri,))
            kb.op("dve", lambda e: e.tensor_copy(out=R[:], in_=Ri[:]), reads=(Ri,), writes=(R,))
            ns = len(steps)
            Dt = kb.sb("Dt", [P, ns * 4], F32)
            for j, (t, bp, bn) in enumerate(steps):
                kb.op("dve", lambda e: e.tensor_tensor(out=Dt[:, j * 4:(j + 1) * 4], in0=self.rbb[:, bn * 4:(bn + 1) * 4],
                                                       in1=self.rbb[:, bp * 4:(bp + 1) * 4], op=ALU.subtract),
                      reads=(self.rbb,), writes=(Dt,))
            accs = [kb.sb("acc", [P, XW], F32) for _ in range(4)]
            msk = [kb.sb("smask", [P, XW], F32) for _ in range(2)]
            for h in range(4):
                kb.op("dve", lambda e: e.tensor_scalar(out=accs[h][:], in0=R[:], scalar1=0.0,
                                                       scalar2=self.rbb[:, first * 4 + h:first * 4 + h + 1],
                                                       op0=ALU.mult, op1=ALU.add),
                      reads=(R, self.rbb), writes=(accs[h],))
            for j, (t, bp, bn) in enumerate(steps):
                mk = msk[j % 2]
                kb.op("dve", lambda e: e.tensor_scalar(out=mk[:], in0=R[:], scalar1=float(t), scalar2=None, op0=ALU.is_ge),
                      reads=(R,), writes=(mk,))
                for h in range(4):
                    kb.op("dve", lambda e: e.scalar_tensor_tensor(out=accs[h][:], in0=mk[:], scalar=Dt[:, j * 4 + h:j * 4 + h + 1],
                                                                  in1=accs[h][:], op0=ALU.mult, op1=ALU.add),
                          reads=(mk, Dt, accs[h]), writes=(accs[h],))
            for h in range(4):
                kb.dma("sp", [(self.strip_d[h], accs[h][:])], reads=(accs[h],))

    def phase_e1(self, l, src):
        kb = self.kb
        e = l // 2
        with kb.phase():
            W = self.norm_work()
            g_mix = self.load_gain(self.norm_mix[l:l + 1, :], "g_mix")
            wi = kb.sb("wi", [P, 8, 4096], BF16)
            kb.dma("sp", [(wi[:], self.wmix_in[l])], reads=(self.wl_mi[l],), writes=(wi,))
            lbt = []
            for d, src_lb in enumerate((self.hg_lb_fwd, self.hg_lb_bwd)):
                lb = kb.sb("lb", [P, 512], F32)
                oml = kb.sb("oml", [P, 512], F32)
                if e == 0:
                    kb.op("dve", lambda e_: e_.memset(lb[:], 0.0), writes=(lb,))
                else:
                    a0 = kb.sb("a0", [P, 512], F32)
                    a1 = kb.sb("a1", [P, 512], F32)
                    kb.dma("sp", [(a0[:], src_lb[0:1, :].partition_broadcast(P))], writes=(a0,))
                    kb.dma("sp", [(a1[:], src_lb[1:2, :].partition_broadcast(P))], writes=(a1,))
                    kb.op("dve", lambda e_: e_.tensor_tensor(out=a1[:], in0=a1[:], in1=a0[:], op=ALU.subtract),
                          reads=(a0, a1), writes=(a1,))
                    kb.op("act", lambda e_: e_.activation(out=lb[:], in_=a1[:], func=AF.Sigmoid), reads=(a1,), writes=(lb,))
                kb.op("dve", lambda e_: e_.tensor_scalar(out=oml[:], in0=lb[:], scalar1=-1.0, scalar2=1.0,
                                                         op0=ALU.mult, op1=ALU.add), reads=(lb,), writes=(oml,))
                lbt.append((lb, oml))
            hblk = [kb.sb("hblk", [P, 4, D], F32) for _ in range(2)]
            hnT = kb.sb("hnT", [P, 8, TB], BF16)
            st_fm = {k: kb.sb("st_" + k, [P, 4, TB], BF16) for k in ("q", "k", "hq", "g")}
            st_v = kb.sb("st_v", [P, 4, 512], BF16)
            st_i = kb.sb("st_i", [P, 4, 512], BF16)
            st_k = [kb.sb("st_kf", [P, 4, 512], BF16) for _ in range(2)]
            st_lf = [kb.sb("st_lf", [P, 4, 512], F32) for _ in range(2)]
            sig = [kb.sb("sig", [P, 512], F32) for _ in range(2)]
            ft = [kb.sb("ft", [P, 512], F32) for _ in range(2)]
            psX = [kb.ps("psX", [P, TB]) for _ in range(4)]
            fm_kinds = (("q", 0, self.qT_d), ("k", 512, self.kT_d), ("hq", 1536, self.hq_d), ("g", 3584, self.g_d))
            self.load_hblk(src, 0, hblk[0])
            nx = 0
            nz = 0
            for i in range(NBLK):
                if i + 1 < NBLK:
                    self.load_hblk(src, i + 1, hblk[(i + 1) % 2])
                hb = hblk[i % 2]
                for s in range(4):
                    self.norm_T(hb[:, s, :], hb, g_mix, hnT, s * P, W)
                for kind, c0, dram in fm_kinds:
                    st = st_fm[kind]
                    for h in range(4):
                        px = psX[nx % 4]
                        nx += 1
                        col = c0 + h * P
                        for kc in range(8):
                            kb.op("pe", lambda e_: e_.matmul(px[:], lhsT=wi[:, kc, col:col + P], rhs=hnT[:, kc, :],
                                                             start=(kc == 0), stop=(kc == 7)),
                                  reads=(wi, hnT), writes=(px,), sig=(kc == 7))
                        if kind == "q":
                            kb.op("dve", lambda e_: e_.tensor_scalar(out=st[:, h, :], in0=px[:], scalar1=0.125, scalar2=None,
                                                                     op0=ALU.mult), reads=(px,), writes=(st,))
                        elif kind == "k":
                            kb.op("dve", lambda e_: e_.tensor_copy(out=st[:, h, :], in_=px[:]), reads=(px,), writes=(st,))
                        else:
                            kb.op("act", lambda e_: e_.activation(out=st[:, h, :], in_=px[:], func=AF.Silu),
                                  reads=(px,), writes=(st,))
                    kb.dma("sp", [(dram.rearrange("h p t -> p h t")[:, :, i * TB:(i + 1) * TB], st[:])], reads=(st,))
                for s in range(4):
                    for kind, c0 in (("v", 1024), ("i", 3072), ("zf", 2048), ("zb", 2560)):
                        px = psX[nx % 4]
                        nx += 1
                        for kc in range(8):
                            kb.op("pe", lambda e_: e_.matmul(px[:], lhsT=hnT[:, kc, s * P:(s + 1) * P], rhs=wi[:, kc, c0:c0 + 512],
                                                             start=(kc == 0), stop=(kc == 7)),
                                  reads=(hnT, wi), writes=(px,), sig=(kc == 7))
                        if kind in ("v", "i"):
                            st = st_v if kind == "v" else st_i
                            kb.op("dve", lambda e_: e_.tensor_copy(out=st[:, s, :], in_=px[:]), reads=(px,), writes=(st,))
                        else:
                            d = 0 if kind == "zf" else 1
                            lb, oml = lbt[d]
                            sg = sig[nz % 2]
                            f_ = ft[nz % 2]
                            nz += 1
                            kb.op("act", lambda e_: e_.activation(out=sg[:], in_=px[:], func=AF.Sigmoid), reads=(px,), writes=(sg,))
                            if e == 0:
                                fsrc = sg
                            else:
                                kb.op("dve", lambda e_: e_.tensor_tensor(out=f_[:], in0=sg[:], in1=oml[:], op=ALU.mult),
                                      reads=(sg, oml), writes=(f_,))
                                kb.op("pool", lambda e_: e_.tensor_tensor(out=f_[:], in0=f_[:], in1=lb[:], op=ALU.add),
                                      reads=(f_, lb), writes=(f_,))
                                fsrc = f_
                            kb.op("act", lambda e_: e_.activation(out=st_lf[d][:, s, :], in_=fsrc[:], func=AF.Ln),
                                  reads=(fsrc,), writes=(st_lf[d],))
                            kb.op("pool", lambda e_: e_.tensor_scalar(out=st_k[d][:, s, :], in0=fsrc[:], scalar1=-1.0, scalar2=1.0,
                                                                      op0=ALU.mult, op1=ALU.add), reads=(fsrc,), writes=(st_k[d],))
                rows = slice(i * TB, (i + 1) * TB)
                kb.dma("sp", [(self.v_d[rows, :].rearrange("(s p) c -> p s c", p=P), st_v[:])], reads=(st_v,))
                kb.dma("sp", [(self.i_d[rows, :].rearrange("(s p) c -> p s c", p=P), st_i[:])], reads=(st_i,))
                for d in range(2):
                    kb.dma("sp", [(self.kf_d[d, rows, :].rearrange("(s p) c -> p s c", p=P), st_k[d][:])], reads=(st_k[d],))
                    kb.dma("sp", [(self.lf_d[d, rows, :].rearrange("(s p) c -> p s c", p=P), st_lf[d][:])], reads=(st_lf[d],))

    def phase_e2(self, l):
        kb = self.kb
        e = l // 2
        lambda_init = 0.8 - 0.6 * math.exp(-0.3 * l)
        with kb.phase():
            strips = []
            for h in range(4):
                sf = kb.sb("strip_f", [P, 1152], F32)
                kb.dma("sp", [(sf[:], self.strip_d[h])], writes=(sf,))
                strips.append(sf)
            lv = []
            for nm, ap in (("q1", self.lambda_q1), ("k1", self.lambda_k1), ("q2", self.lambda_q2), ("k2", self.lambda_k2)):
                t = kb.sb("l" + nm, [P, 64], F32)
                kb.dma("sp", [(t[:], ap[e:e + 1, :].partition_broadcast(P))], writes=(t,))
                lv.append(t)
            es = []
            for a, b in ((lv[0], lv[1]), (lv[2], lv[3])):
                sm = kb.sb("lsum", [P, 1], F32)
                ex = kb.sb("lexp", [P, 1], F32)
                kb.op("dve", lambda e_: e_.tensor_tensor(out=a[:], in0=a[:], in1=b[:], op=ALU.mult), reads=(a, b), writes=(a,))
                kb.op("dve", lambda e_: e_.reduce_sum(out=sm[:], in_=a[:], axis=mybir.AxisListType.X), reads=(a,), writes=(sm,))
                kb.op("act", lambda e_: e_.activation(out=ex[:], in_=sm[:], func=AF.Exp), reads=(sm,), writes=(ex,))
                es.append(ex)
            neglam = kb.sb("neglam", [P, 1], F32)
            kb.op("dve", lambda e_: e_.tensor_tensor(out=neglam[:], in0=es[1][:], in1=es[0][:], op=ALU.subtract),
                  reads=(es[0], es[1]), writes=(neglam,))
            kb.op("dve", lambda e_: e_.tensor_scalar(out=neglam[:], in0=neglam[:], scalar1=-lambda_init, scalar2=None, op0=ALU.add),
                  reads=(neglam,), writes=(neglam,))
            sublnb = kb.sb("sublnb", [P, P], F32)
            kb.dma("sp", [(sublnb[:], self.da_subln[e:e + 1, :].partition_broadcast(P))], writes=(sublnb,))
            kb.op("dve", lambda e_: e_.tensor_scalar(out=sublnb[:], in0=sublnb[:], scalar1=1.0 - lambda_init, scalar2=None, op0=ALU.mult),
                  reads=(sublnb,), writes=(sublnb,))
            VW = 132
            qT = [kb.sb("qT", [P, S], BF16) for _ in range(2)]
            ROWTILE = self.debug.get("e2_rowtile", False)
            kTu = [kb.sb("kTu", [P, S], BF16) for _ in range(2)] if ROWTILE else None
            kTa = [kb.sb("kTa", [P, S], BF16) for _ in range(2)]
            kTb = [kb.sb("kTb", [P, S], BF16) for _ in range(2)]
            V = [kb.sb("V", [P, 32, VW], BF16) for _ in range(2)]
            for b_ in range(2):
                kb.op("dve", lambda e_: e_.memset(kTa[b_][64:128, :], 0.0), writes=(kTa[b_],))
                kb.op("dve", lambda e_: e_.memset(kTb[b_][0:64, :], 0.0), writes=(kTb[b_],))
                kb.op("dve", lambda e_: e_.memset(V[b_][:, :, 128:VW], 1.0), writes=(V[b_],))
            P12 = [kb.sb("P12", [P, 2, TB], BF16) for _ in range(4)]
            stmp = [kb.sb("stmp", [P, 2, TB], F32) for _ in range(2)]
            rz = [kb.sb("rz", [P, 4], F32) for _ in range(8)]
            At = [kb.sb("At", [P, P], F32) for _ in range(2)]
            Dt_ = [kb.sb("Dt_", [P, P], F32) for _ in range(8)]
            sqj = kb.sb("sqj", [P, P], BF16)
            ssq = [kb.sb("ssq", [P, 1], F32) for _ in range(8)]
            deferred = []
            no_box = [0]
            rt_ = [kb.sb("rt_", [P, 1], F32) for _ in range(2)]
            Dn = [kb.sb("Dn", [P, P], BF16) for _ in range(2)]
            ostg = [kb.sb("ostg", [P, 4, P], BF16) for _ in range(2)]
            psS = [kb.ps("psS", [P, 2, TB]) for _ in range(2)]
            acc = [kb.ps("acc", [P, TB]) for _ in range(3)]
            pT = kb.ps("pT", [P, 4, P], BF16)
            cneg = self.b_first
            cpos = self.b_last
            combos = [(sq, h) for sq in range(NSEQ) for h in range(4)][:self.debug.get('e2_combos', 8)]
            NQT = self.debug.get('e2_qts', 8)

            def region(j, s_):
                idx = j * 2 + s_
                return acc[idx // 3], (idx % 3) * 129, idx

            def load(ci):
                sq, h = combos[ci]
                b = ci % 2
                kb.dma("sp", [(qT[b][:], self.qT_d[h, :, sq * S:(sq + 1) * S])], writes=(qT[b],))
                if ROWTILE:
                    kb.dma("sp", [(kTu[b][:], self.kT_d[h, :, sq * S:(sq + 1) * S])], writes=(kTu[b],))
                kb.dma("sp", [(kTa[b][0:64, :], self.kT_d[h, 0:64, sq * S:(sq + 1) * S])], writes=(kTa[b],))
                kb.dma("sp", [(kTb[b][64:128, :], self.kT_d[h, 64:128, sq * S:(sq + 1) * S])], writes=(kTb[b],))
                kb.dma("sp", [(V[b][:, kb4 * 8:(kb4 + 1) * 8, 0:P],
                               self.v_d[sq * S + kb4 * 1024:sq * S + (kb4 + 1) * 1024, h * P:(h + 1) * P].rearrange("(kb p) c -> p kb c", p=P))
                              for kb4 in range(4)], writes=(V[b],))

            def epilogue2(items, h_, t0):
                nonlocal_no = no_box
                og = ostg[nonlocal_no[0] % 2]
                nonlocal_no[0] += 1
                for (x, j) in items:
                    rz_, D_, sq_, rtt_, dn_ = rz[x], Dt_[x], ssq[x], rt_[x % 2], Dn[x % 2]
                    kb.op("act", lambda e_: e_.activation(out=rtt_[:], in_=sq_[:], func=AF.Ln, bias=self.eps_col[:], scale=1.0 / P),
                          reads=(sq_, self.eps_col), writes=(rtt_,))
                    kb.op("act", lambda e_: e_.activation(out=rz_[:, 3:4], in_=rtt_[:], func=AF.Exp, scale=-0.5),
                          reads=(rtt_,), writes=(rz_,))
                    kb.op("dve", lambda e_: e_.scalar_tensor_tensor(out=dn_[:], in0=D_[:], scalar=rz_[:, 3:4], in1=sublnb[:],
                                                                    op0=ALU.mult, op1=ALU.mult),
                          reads=(D_, rz_, sublnb), writes=(dn_,))
                    kb.op("pe", lambda e_: e_.transpose(out=pT[:, j, :], in_=dn_[:], identity=self.ident_b[:]),
                          reads=(dn_, self.ident_b), writes=(pT,))
                kb.op("act", lambda e_: e_.copy(out=og[:], in_=pT[:]), reads=(pT,), writes=(og,))
                kb.dma("sp", [(self.o_d[h_ * P:(h_ + 1) * P, t0:t0 + TB], og[:].rearrange("p j t -> p (j t)"))], reads=(og,))


            load(0)
            nS = 0
            no = 0
            nj = 0
            for ci, (sq, h) in enumerate(combos):
                if ci + 1 < len(combos):
                    load(ci + 1)
                q_, ka_, kb__, v_ = qT[ci % 2], kTa[ci % 2], kTb[ci % 2], V[ci % 2]
                sf = strips[h]
                nmx = [0]

                def emit_S(qt, kbk, n):
                    ps_ = psS[n % 2]
                    d0 = kbk - 4 * qt
                    mixed = (-1 <= d0 <= 4)
                    if ROWTILE and not mixed:
                        ku_ = kTu[ci % 2]
                        for (si, r0) in ((0, 0), (1, 64)):
                            kb.op("pe", lambda e_: e_.matmul(ps_[:, si, :], lhsT=ku_[r0:r0 + 64, kbk * P:(kbk + 1) * P],
                                                             rhs=q_[r0:r0 + 64, qt * TB:(qt + 1) * TB], start=True, stop=True),
                                  reads=(ku_, q_), writes=(ps_,), sig=(si == 1))
                    else:
                        for (si, kp) in ((0, ka_), (1, kb__)):
                            kb.op("pe", lambda e_: e_.matmul(ps_[:, si, :], lhsT=kp[:, kbk * P:(kbk + 1) * P],
                                                             rhs=q_[:, qt * TB:(qt + 1) * TB], start=True, stop=True),
                                  reads=(kp, q_), writes=(ps_,), sig=(si == 1))
                    if not mixed:
                        if d0 < -1:
                            return ps_, self.rbb, cneg * 4 + h
                        return ps_, self.rbb, cpos * 4 + h
                    m = 4 - d0
                    tm = stmp[nmx[0] % 2]
                    nmx[0] += 1
                    for si in range(2):
                        kb.op("dve", lambda e_: e_.tensor_tensor(out=tm[:, si, :], in0=ps_[:, si, :], in1=sf[:, m * P:m * P + TB], op=ALU.add),
                              reads=(ps_, sf), writes=(tm,))
                    return tm, self.zero_col, 0

                def emit_PV(pp_, kbk, pos):
                    for j in range(4):
                        for s_ in range(2):
                            bank, off, idx = region(j, s_)
                            kb.op("pe", lambda e_: e_.matmul(bank[:, off:off + 129], lhsT=pp_[:, s_, j * P:(j + 1) * P],
                                                             rhs=v_[:, kbk, 0:129], start=(pos == 0 and idx % 3 == 0),
                                                             stop=(pos == 31), skip_group_check=True),
                                  reads=(pp_, v_), writes=(bank,), sig=(pos == 31 and idx == 7))

                for qt in range(NQT):
                    order = list(range(32))
                    assert sorted(order) == list(range(32))
                    pend = emit_S(qt, order[0], nS)
                    prev = None
                    for pos in range(32):
                        n = nS
                        nS += 1
                        cur = pend
                        if pos + 1 < 32:
                            pend = emit_S(qt, order[pos + 1], nS)
                        src_, cb, cc = cur
                        pp_ = P12[n % 4]
                        kb.op("act", lambda e_: e_.activation(out=pp_[:], in_=src_[:], func=AF.Exp, bias=cb[:, cc:cc + 1], scale=1.0),
                              reads=(src_, cb), writes=(pp_,))
                        if prev is not None:
                            emit_PV(*prev)
                        prev = (pp_, order[pos], pos)
                        if pos == 3 and deferred:
                            epilogue2(*deferred.pop(0))
                    emit_PV(*prev)
                    items = []
                    for j in range(4):
                        b1, o1, _ = region(j, 0)
                        b2, o2, _ = region(j, 1)
                        x = nj % 8
                        nj += 1
                        rz_, A_, D_, sq_ = rz[x], At[x % 2], Dt_[x], ssq[x]
                        kb.op("dve", lambda e_: e_.reciprocal(out=rz_[:, 0:1], in_=b1[:, o1 + 128:o1 + 129]), reads=(b1,), writes=(rz_,))
                        kb.op("dve", lambda e_: e_.reciprocal(out=rz_[:, 1:2], in_=b2[:, o2 + 128:o2 + 129]), reads=(b2,), writes=(rz_,))
                        kb.op("dve", lambda e_: e_.tensor_tensor(out=rz_[:, 2:3], in0=rz_[:, 1:2], in1=neglam[:], op=ALU.mult),
                              reads=(rz_, neglam), writes=(rz_,))
                        kb.op("dve", lambda e_: e_.tensor_scalar(out=A_[:], in0=b1[:, o1:o1 + 128], scalar1=rz_[:, 0:1], scalar2=None, op0=ALU.mult),
                              reads=(b1, rz_), writes=(A_,))
                        kb.op("dve", lambda e_: e_.scalar_tensor_tensor(out=D_[:], in0=b2[:, o2:o2 + 128], scalar=rz_[:, 2:3], in1=A_[:],
                                                                        op0=ALU.mult, op1=ALU.add),
                              reads=(b2, rz_, A_), writes=(D_,))
                        kb.op("dve", lambda e_: e_.scalar_tensor_tensor(out=sqj[:], in0=D_[:], scalar=1.0, in1=D_[:],
                                                                        op0=ALU.mult, op1=ALU.mult, accum_out=sq_[:]),
                              reads=(D_,), writes=(sqj, sq_))
                        items.append((x, j))
                    deferred.append((items, h, sq * S + qt * TB))

            while deferred:
                epilogue2(*deferred.pop(0))

    def phase_e3(self, l):
        kb = self.kb
        e = l // 2
        with kb.phase():
            tri = kb.sb("tri", [P, 6, P], F32)
            kb.dma("sp", [(tri[:], self.c_tri.rearrange("k a b -> a k b"))], writes=(tri,))
            mask = kb.sb("mask", [P, 2, 512], F32)
            kb.dma("sp", [(mask[:], self.c_mask.rearrange("d p m -> p d m"))], writes=(mask,))
            hgn = kb.sb("hgn", [P, 1], F32)
            kb.dma("sp", [(hgn[:], self.hg_norm[e:e + 1, :].rearrange("o d -> d o"))], writes=(hgn,))
            Sf = [kb.sb("Sf", [P, 4, P], F32) for _ in range(NSEQ)]
            Sb = [kb.sb("Sb", [P, 4, P], BF16) for _ in range(NSEQ)]
            def mk(name, shape, dt, cnt):
                return [kb.sb(name, shape, dt) for _ in range(cnt)]

            def B(lst, n):
                return lst[n % len(lst)]

            lf = mk("lf", [P, 512], F32, 4)
            kk = mk("kk", [P, 512], BF16, 4)
            hq = mk("hq", [P, 4, P], BF16, 4)
            ii = mk("ii", [P, 512], BF16, 8)
            gt = mk("gt", [P, 4, P], BF16, 8)
            ofw = mk("ofw", [P, 512], F32, 8)
            E1t = mk("E1t", [P, 512], F32, 2)
            E2t = mk("E2t", [P, 512], F32, 2)
            E3t = mk("E3t", [P, 512], F32, 2)
            decs = mk("decs", [P, 2, 4], F32, 6)
            kbar = mk("kbar", [P, 512], BF16, 4)
            kbc = [[kb.sb("kbc", [P, 512], BF16) for _ in range(2)] for _ in range(6)]
            for y_ in range(len(kbc)):
                for c_ in range(2):
                    kb.op("dve", lambda e_: e_.memset(kbc[y_][c_][:], 0.0), writes=(kbc[y_][c_],))
            qtil = mk("qtil", [P, 4, P], BF16, 6)
            qhat = mk("qhat", [P, 4, P], BF16, 5)
            khT = mk("khT", [P, 4, P], BF16, 4)
            attm = mk("attm", [P, 512], BF16, 4)
            ofs = mk("ofs", [P, 512], F32, 2)
            tot = kb.sb("tot", [P, 512], F32)
            sqb = kb.sb("sqb", [P, 512], BF16)
            rtt = kb.sb("rtt", [P, 512], F32)
            t2 = kb.sb("t2", [P, 512], F32)
            obs = mk("obs", [P, 4, P], BF16, 2)
            pp = [kb.ps("pp", [P, 512]) for _ in range(2)]
            p_kT = kb.ps("p_kT", [P, 4, P], BF16)
            p_att = kb.ps("p_att", [P, 512])
            p_O = [kb.ps("p_O", [P, 512]) for _ in range(2)]
            p_S = [kb.ps("p_S", [P, 512]) for _ in range(2)]
            npp = [0]

            def nextpp():
                b_ = pp[npp[0] % len(pp)]
                npp[0] += 1
                return b_

            for d in range(2):
                if d == 1:
                    kb.barrier()
                for sq in range(NSEQ):
                    kb.op("dve", lambda e_: e_.memset(Sf[sq][:], 0.0), writes=(Sf[sq],))
                    kb.op("dve", lambda e_: e_.memset(Sb[sq][:], 0.0), writes=(Sb[sq],))
                seqn = [(st if d == 0 else 31 - st, sq) for st in range(32) for sq in range(NSEQ)]
                NN = len(seqn)

                def load(n):
                    b, sq = seqn[n]
                    t0 = sq * S + b * P
                    kb.dma("sp", [(B(lf, n)[:], self.lf_d[d, t0:t0 + P, :])], writes=(B(lf, n),))
                    kb.dma("sp", [(B(kk, n)[:], self.kf_d[d, t0:t0 + P, :])], writes=(B(kk, n),))
                    kb.dma("sp", [(B(ii, n)[:], self.i_d[t0:t0 + P, :])], writes=(B(ii, n),))
                    kb.dma("sp", [(B(hq, n)[:], self.hq_d.rearrange("h p t -> p h t")[:, :, t0:t0 + P])], writes=(B(hq, n),))
                    if d == 1:
                        kb.dma("sp", [(B(gt, n)[:], self.g_d.rearrange("h p t -> p h t")[:, :, t0:t0 + P])], writes=(B(gt, n),))
                        kb.dma("sp", [(B(ofw, n)[:], self.of_d[t0 // P])], writes=(B(ofw, n),))

                def st1(n):
                    lf_, kk_, hq_ = B(lf, n), B(kk, n), B(hq, n)
                    e1, e2, e3 = B(E1t, n), B(E2t, n), B(E3t, n)
                    kb_, qt_, qh_, kc2, dc = B(kbar, n), B(qtil, n), B(qhat, n), B(kbc, n), B(decs, n)
                    p_rev = nextpp()
                    kb.op("pe", lambda e_: e_.matmul(p_rev[:], lhsT=tri[:, 1 + 3 * d, :], rhs=lf_[:], start=True, stop=True),
                          reads=(tri, lf_), writes=(p_rev,))
                    kb.op("act", lambda e_: e_.activation(out=e3[:], in_=p_rev[:], func=AF.Exp), reads=(p_rev,), writes=(e3,))
                    kb.op("dve", lambda e_: e_.tensor_tensor(out=kb_[:], in0=kk_[:], in1=e3[:], op=ALU.mult),
                          reads=(kk_, e3), writes=(kb_,))
                    for c in range(2):
                        kb.op("pool", lambda e_: e_.tensor_copy(out=kc2[c][c * 64:(c + 1) * 64, :], in_=kb_[c * 64:(c + 1) * 64, :]),
                              reads=(kb_,), writes=(kc2[c],))
                    p_cum = nextpp()
                    for h in range(4):
                        kb.op("pe", lambda e_: e_.matmul(p_cum[:, h * P:(h + 1) * P], lhsT=lf_[:, h * P:(h + 1) * P],
                                                         rhs=tri[:, 0 + 3 * d, :], start=True, stop=True),
                              reads=(tri, lf_), writes=(p_cum,), sig=(h == 3))
                    kb.op("act", lambda e_: e_.activation(out=e1[:], in_=p_cum[:], func=AF.Exp), reads=(p_cum,), writes=(e1,))
                    e1v = e1[:].rearrange("p (h t) -> p h t", h=4)
                    kb.op("dve", lambda e_: e_.tensor_tensor(out=qt_[:], in0=hq_[:], in1=e1v, op=ALU.mult),
                          reads=(hq_, e1), writes=(qt_,))
                    for c in range(2):
                        tl = c * 64 + 63 if d == 0 else c * 64
                        kb.op("dve", lambda e_: e_.tensor_copy(out=dc[:, c, :], in_=e1v[:, :, tl]),
                              reads=(e1,), writes=(dc,))
                    p_revT = nextpp()
                    for h in range(4):
                        kb.op("pe", lambda e_: e_.matmul(p_revT[:, h * P:(h + 1) * P], lhsT=lf_[:, h * P:(h + 1) * P],
                                                         rhs=tri[:, 2 + 3 * d, :], start=True, stop=True),
                              reads=(tri, lf_), writes=(p_revT,), sig=(h == 3))
                    kb.op("act", lambda e_: e_.activation(out=e2[:], in_=p_revT[:], func=AF.Exp), reads=(p_revT,), writes=(e2,))
                    kb.op("pool", lambda e_: e_.tensor_tensor(out=qh_[:], in0=hq_[:], in1=e2[:].rearrange("p (h t) -> p h t", h=4), op=ALU.mult),
                          reads=(hq_, e2), writes=(qh_,))

                def st2(n):
                    kb_, kh_ = B(kbar, n), B(khT, n)
                    for h in range(4):
                        kb.op("pe", lambda e_: e_.transpose(out=p_kT[:, h, :], in_=kb_[:, h * P:(h + 1) * P], identity=self.ident_b[:]),
                              reads=(kb_, self.ident_b), writes=(p_kT,), sig=(h == 3))
                    kb.op("act", lambda e_: e_.copy(out=kh_[:], in_=p_kT[:]), reads=(p_kT,), writes=(kh_,))

                def st3(n):
                    kh_, qh_, am = B(khT, n), B(qhat, n), B(attm, n)
                    for h in range(4):
                        kb.op("pe", lambda e_: e_.matmul(p_att[:, h * P:(h + 1) * P], lhsT=kh_[:, h, :], rhs=qh_[:, h, :],
                                                         start=True, stop=True),
                              reads=(kh_, qh_), writes=(p_att,), sig=(h == 3))
                    kb.op("dve", lambda e_: e_.tensor_tensor(out=am[:], in0=p_att[:], in1=mask[:, d, :], op=ALU.mult),
                          reads=(p_att, mask), writes=(am,))

                def chain2(ns):
                    for n in ns:
                        ii_, am, po = B(ii, n), B(attm, n), p_O[n % 2]
                        for h in range(4):
                            kb.op("pe", lambda e_: e_.matmul(po[:, h * P:(h + 1) * P], lhsT=ii_[:, h * P:(h + 1) * P],
                                                             rhs=am[:, h * P:(h + 1) * P], start=(h == 0), stop=False, skip_group_check=True),
                                  reads=(ii_, am), writes=(po,), sig=False)
                    order = (0, 1) if d == 0 else (1, 0)
                    for ci_, c in enumerate(order):
                        r0 = c * 64
                        for n in ns:
                            b, sq = seqn[n]
                            ii_, qt_, po, dc, ps_ = B(ii, n), B(qtil, n), p_O[n % 2], B(decs, n), p_S[sq]
                            kc_ = B(kbc, n)[c]
                            for h in range(4):
                                kb.op("pe", lambda e_: e_.matmul(po[:, h * P + r0:h * P + r0 + 64], lhsT=Sb[sq][:, h, :],
                                                                 rhs=qt_[:, h, r0:r0 + 64], start=False, stop=(ci_ == 1),
                                                                 skip_group_check=True),
                                      reads=(Sb[sq], qt_), writes=(po,), sig=(h == 3))
                            for h in range(4):
                                kb.op("pe", lambda e_: e_.matmul(ps_[:, h * P:(h + 1) * P], lhsT=kc_[:, h * P:(h + 1) * P],
                                                                 rhs=ii_[:, h * P:(h + 1) * P], start=True, stop=True),
                                      reads=(kc_, ii_), writes=(ps_,), sig=(h == 3))
                            for h in range(4):
                                kb.op("dve", lambda e_: e_.scalar_tensor_tensor(out=Sf[sq][:, h, :], in0=Sf[sq][:, h, :],
                                                                                scalar=dc[:, c, h:h + 1],
                                                                                in1=ps_[:, h * P:(h + 1) * P], op0=ALU.mult, op1=ALU.add),
                                      reads=(Sf[sq], dc, ps_), writes=(Sf[sq],))
                            kb.op("act", lambda e_: e_.copy(out=Sb[sq][:], in_=Sf[sq][:]), reads=(Sf[sq],), writes=(Sb[sq],))

                def tail(n):
                    b, sq = seqn[n]
                    po = p_O[n % 2]
                    t0 = sq * S + b * P
                    if d == 0:
                        of_ = ofs[n % 2]
                        kb.op("act", lambda e_: e_.copy(out=of_[:], in_=po[:]), reads=(po,), writes=(of_,))
                        kb.dma("sp", [(self.of_d[t0 // P], of_[:])], reads=(of_,))
                    else:
                        ob = obs[n % 2]
                        ofw_, gt_ = B(ofw, n), B(gt, n)
                        kb.op("dve", lambda e_: e_.tensor_tensor(out=tot[:], in0=po[:], in1=ofw_[:], op=ALU.add),
                              reads=(po, ofw_), writes=(tot,))
                        kb.op("act", lambda e_: e_.activation(out=sqb[:], in_=tot[:], func=AF.Square), reads=(tot,), writes=(sqb,))
                        p_ss = nextpp()
                        kb.op("pe", lambda e_: e_.matmul(p_ss[:], lhsT=self.ones_b[:], rhs=sqb[:], start=True, stop=True),
                              reads=(self.ones_b, sqb), writes=(p_ss,))
                        kb.op("act", lambda e_: e_.activation(out=rtt[:], in_=p_ss[:], func=AF.Ln, bias=self.eps_col[:], scale=1.0 / P),
                              reads=(p_ss, self.eps_col), writes=(rtt,))
                        kb.op("act", lambda e_: e_.activation(out=rtt[:], in_=rtt[:], func=AF.Exp, scale=-0.5), reads=(rtt,), writes=(rtt,))
                        kb.op("dve", lambda e_: e_.scalar_tensor_tensor(out=t2[:], in0=tot[:], scalar=hgn[:, 0:1], in1=rtt[:],
                                                                        op0=ALU.mult, op1=ALU.mult),
                              reads=(tot, hgn, rtt), writes=(t2,))
                        kb.op("pool", lambda e_: e_.tensor_tensor(out=ob[:], in0=t2[:].rearrange("p (h t) -> p h t", h=4), in1=gt_[:], op=ALU.mult),
                              reads=(t2, gt_), writes=(ob,))
                        kb.dma("sp", [(self.o_d[512:1024, t0:t0 + P].rearrange("(h v) t -> v h t", v=P), ob[:])], reads=(ob,))

                for n0 in range(4):
                    load(n0)
                st1(0); st1(1); st1(2)
                st2(0); st2(1)
                st3(0)
                for n in range(NN):
                    if n + 4 < NN:
                        load(n + 4)
                    if n + 3 < NN:
                        st1(n + 3)
                    if n + 2 < NN:
                        st2(n + 2)
                    if n + 1 < NN:
                        st3(n + 1)
                    if n % 2 == 1:
                        chain2((n - 1, n))
                        tail(n - 1)
                        tail(n)


def _run(inputs, prog, core_ids):
    consts = _host_consts()
    x = np.ascontiguousarray(inputs["x"], dtype=np.float32)
    in_maps = []
    for c in core_ids:
        m = {"x": x[2 * c:2 * c + 2].reshape(NT, D)}
        for k, v in inputs.items():
            if k == "x":
                continue
            a = np.ascontiguousarray(v, dtype=np.float32)
            if k == "norm_final":
                a = a.reshape(1, D)
            m[k] = a
        m.update(consts)
        in_maps.append(m)
    return run_bass_kernel_spmd(prog.nc, in_maps, core_ids=list(core_ids))


def kernel(**inputs):
    prog = Prog()
    prog.build()
    res = _run(inputs, prog, list(range(NCORES)))
    out = np.concatenate([r["out"].reshape(2, S, D) for r in res.results], axis=0)
    return out.astype(np.float32)
```
